# Optimizing a Trainium2 kernel written in Bass

```python
import math
import jax, jax.numpy as jnp
from jax import lax
import numpy as np

D_MODEL = 1024
BATCH = 4
SEQ = 8192
DEPTH = 4

N_MIXERS = 4
D_FF = 2816
NORM_EPS = 1e-6
N_SUBLAYERS = 3
N_ADA = 3 * N_SUBLAYERS
POOL_WINDOWS = (2, 4, 8, 16)
POOL_GROUP = D_MODEL // len(POOL_WINDOWS)
POOL_MAX_W = max(POOL_WINDOWS)
FOX_HEADS = 16
FOX_HEAD_DIM = D_MODEL // FOX_HEADS
FOX_BLOCK = 128
S5_GROUP = 16
S5_GROUPS = D_MODEL // S5_GROUP
S5_STATE = 64
S5_DT_MIN = 1e-3
S5_DT_MAX = 1e-1
CONV_WIDTH = 3
LAYERS_PER_MIXER = tuple(len(range(m, DEPTH, N_MIXERS)) for m in range(N_MIXERS))

kernel_name = "hybrid_pool_fox_s5_conv_macaron"


def rms_norm(x, gain):
    xf = x.astype(jnp.float32)
    y = xf * lax.rsqrt(jnp.mean(xf * xf, axis=-1, keepdims=True) + NORM_EPS)
    return (y * gain.astype(jnp.float32)).astype(x.dtype)


def adaln(x, gain, shift, scale):
    return rms_norm(x, gain) * (1.0 + scale[:, None, :]) + shift[:, None, :]


def swiglu(h, w_in, w_out):
    g, u = jnp.split(h @ w_in, 2, axis=-1)
    return (jax.nn.silu(g) * u) @ w_out


def pool_mixer(h, w_grp, scale):
    b_, s_, d_ = h.shape
    hf = h.astype(jnp.float32)
    cs = jnp.cumsum(hf, axis=1)
    cs_pad = jnp.pad(cs, ((0, 0), (POOL_MAX_W, 0), (0, 0)))
    pos = jnp.arange(s_)
    outs = []
    for gi, w in enumerate(POOL_WINDOWS):
        sl = slice(gi * POOL_GROUP, (gi + 1) * POOL_GROUP)
        prev = cs_pad[:, POOL_MAX_W - w:POOL_MAX_W - w + s_, sl]
        cnt = jnp.minimum(pos + 1, w).astype(jnp.float32)[None, :, None]
        outs.append((cs[..., sl] - prev) / cnt - hf[..., sl])
    pooled = jnp.stack(outs, axis=2)
    mixed = jnp.einsum('bsgc,gcd->bsgd', pooled, w_grp.astype(jnp.float32))
    return (mixed.reshape(b_, s_, d_) * scale.astype(jnp.float32)).astype(h.dtype)


def fox_attention(h, w_in, b_f, q_gain, k_gain, w_o):
    b_, s_, d_ = h.shape
    proj = h @ w_in
    q, k, v, f_logit = jnp.split(proj, [d_, 2 * d_, 3 * d_], axis=-1)
    q = rms_norm(q.reshape(b_, s_, FOX_HEADS, FOX_HEAD_DIM), q_gain) * (FOX_HEAD_DIM ** -0.5)
    k = rms_norm(k.reshape(b_, s_, FOX_HEADS, FOX_HEAD_DIM), k_gain)
    v = v.reshape(b_, s_, FOX_HEADS, FOX_HEAD_DIM)
    q, k, v = (t.transpose(0, 2, 1, 3) for t in (q, k, v))
    log_f = jax.nn.log_sigmoid((f_logit + b_f).astype(jnp.float32))
    cum_f = jnp.cumsum(log_f, axis=1).transpose(0, 2, 1)
    q_idx = jnp.arange(FOX_BLOCK)
    outs = []
    for blk in range(s_ // FOX_BLOCK):
        q0 = blk * FOX_BLOCK
        kv_end = q0 + FOX_BLOCK
        s = jnp.einsum('bhqd,bhkd->bhqk', q[:, :, q0:kv_end], k[:, :, :kv_end]).astype(jnp.float32)
        s = s + cum_f[:, :, q0:kv_end, None] - cum_f[:, :, None, :kv_end]
        mask = (q0 + q_idx)[:, None] >= jnp.arange(kv_end)[None, :]
        p = jax.nn.softmax(jnp.where(mask, s, -jnp.inf), axis=-1)
        outs.append(jnp.einsum('bhqk,bhkd->bhqd', p.astype(v.dtype), v[:, :, :kv_end]))
    o = jnp.concatenate(outs, axis=2).transpose(0, 2, 1, 3).reshape(b_, s_, d_)
    return o @ w_o


def s5_mixer(h, lam_re, lam_im, log_dt, b_re, b_im, c_re, c_im, d_skip, w_glu):
    f32 = jnp.float32
    b_, s_, d_ = h.shape
    u = h.astype(f32).reshape(b_, s_, S5_GROUPS, S5_GROUP)
    dt = jnp.exp(log_dt.astype(f32))[:, None]
    ar, ai = lam_re.astype(f32), lam_im.astype(f32)
    mag = jnp.exp(ar * dt)
    lb_re, lb_im = mag * jnp.cos(ai * dt), mag * jnp.sin(ai * dt)
    den = ar * ar + ai * ai
    nr, ni = lb_re - 1.0, lb_im
    k_re = (nr * ar + ni * ai) / den
    k_im = (ni * ar - nr * ai) / den
    br, bi = b_re.astype(f32), b_im.astype(f32)
    bb_re = k_re[..., None] * br - k_im[..., None] * bi
    bb_im = k_re[..., None] * bi + k_im[..., None] * br
    x_re = jnp.einsum('bsgi,gni->sbgn', u, bb_re)
    x_im = jnp.einsum('bsgi,gni->sbgn', u, bb_im)
    a_re = jnp.broadcast_to(lb_re, (s_, 1) + lb_re.shape)
    a_im = jnp.broadcast_to(lb_im, (s_, 1) + lb_im.shape)

    def combine(e1, e2):
        a1r, a1i, b1r, b1i = e1
        a2r, a2i, b2r, b2i = e2
        return (a1r * a2r - a1i * a2i, a1r * a2i + a1i * a2r,
                a2r * b1r - a2i * b1i + b2r, a2r * b1i + a2i * b1r + b2i)

    _, _, st_re, st_im = lax.associative_scan(combine, (a_re, a_im, x_re, x_im), axis=0)
    y = (jnp.einsum('sbgn,gin->bsgi', st_re, c_re.astype(f32))
         - jnp.einsum('sbgn,gin->bsgi', st_im, c_im.astype(f32)))
    y = y + d_skip.astype(f32).reshape(S5_GROUPS, S5_GROUP) * u
    g = jax.nn.gelu(y.reshape(b_, s_, d_).astype(h.dtype))
    return g * jax.nn.sigmoid(g @ w_glu)


def short_conv_mixer(h, w_in, conv_w, w_out):
    b_gate, c_gate, z = jnp.split(h @ w_in, 3, axis=-1)
    conv = lax.conv_general_dilated(c_gate * z, conv_w, window_strides=(1,),
                                    padding=((CONV_WIDTH - 1, 0),),
                                    dimension_numbers=('NWC', 'WIO', 'NWC'),
                                    feature_group_count=D_MODEL)
    return (b_gate * conv) @ w_out


def setup_inputs(seed: int = 0) -> dict:
    key = jax.random.key(seed)
    ks = iter(jax.random.split(key, 32))
    f32 = jnp.float32
    D = D_MODEL
    n_a, n_b, n_c, n_d = LAYERS_PER_MIXER

    def nrm(shape, std):
        return jax.random.normal(next(ks), shape, f32) * std

    x = nrm((BATCH, SEQ, D), 1.0)
    c = nrm((BATCH, D), 1.0)
    ada_w = nrm((DEPTH, D, N_ADA * D), 0.1 * D ** -0.5)
    ada_b = nrm((DEPTH, N_ADA * D), 0.01)
    norm_g = 1.0 + nrm((DEPTH, N_SUBLAYERS, D), 0.01)
    ffn_w_in = nrm((DEPTH, 2, D, 2 * D_FF), D ** -0.5)
    ffn_w_out = nrm((DEPTH, 2, D_FF, D), D_FF ** -0.5)
    pool_w = nrm((n_a, len(POOL_WINDOWS), POOL_GROUP, POOL_GROUP), POOL_GROUP ** -0.5)
    pool_scale = 1.0 + nrm((n_a, D), 0.02)
    fox_w_in = nrm((n_b, D, 3 * D + FOX_HEADS), D ** -0.5)
    fox_b_f = 2.0 + nrm((n_b, FOX_HEADS), 0.5)
    fox_q_gain = 1.0 + nrm((n_b, FOX_HEAD_DIM), 0.01)
    fox_k_gain = 1.0 + nrm((n_b, FOX_HEAD_DIM), 0.01)
    fox_w_o = nrm((n_b, D, D), D ** -0.5)
    n_idx = jnp.arange(S5_STATE, dtype=f32)
    s5_lam_re = -0.5 + nrm((n_c, S5_GROUPS, S5_STATE), 0.01)
    s5_lam_im = math.pi * n_idx + nrm((n_c, S5_GROUPS, S5_STATE), 0.01)
    s5_log_dt = jax.random.uniform(next(ks), (n_c, S5_GROUPS), f32,
                                   math.log(S5_DT_MIN), math.log(S5_DT_MAX))
    s5_b_re = nrm((n_c, S5_GROUPS, S5_STATE, S5_GROUP), (2 * S5_GROUP) ** -0.5)
    s5_b_im = nrm((n_c, S5_GROUPS, S5_STATE, S5_GROUP), (2 * S5_GROUP) ** -0.5)
    s5_c_re = nrm((n_c, S5_GROUPS, S5_GROUP, S5_STATE), 2.0 * S5_STATE ** -0.5)
    s5_c_im = nrm((n_c, S5_GROUPS, S5_GROUP, S5_STATE), 2.0 * S5_STATE ** -0.5)
    s5_d = nrm((n_c, D), 0.5)
    s5_w_glu = nrm((n_c, D, D), D ** -0.5)
    conv_w_in = nrm((n_d, D, 3 * D), D ** -0.5)
    conv_w = nrm((n_d, CONV_WIDTH, 1, D), CONV_WIDTH ** -0.5)
    conv_w_out = nrm((n_d, D, D), D ** -0.5)
    return {"x": x, "c": c, "ada_w": ada_w, "ada_b": ada_b, "norm_g": norm_g,
            "ffn_w_in": ffn_w_in, "ffn_w_out": ffn_w_out,
            "pool_w": pool_w, "pool_scale": pool_scale,
            "fox_w_in": fox_w_in, "fox_b_f": fox_b_f, "fox_q_gain": fox_q_gain,
            "fox_k_gain": fox_k_gain, "fox_w_o": fox_w_o,
            "s5_lam_re": s5_lam_re, "s5_lam_im": s5_lam_im, "s5_log_dt": s5_log_dt,
            "s5_b_re": s5_b_re, "s5_b_im": s5_b_im, "s5_c_re": s5_c_re, "s5_c_im": s5_c_im,
            "s5_d": s5_d, "s5_w_glu": s5_w_glu,
            "conv_w_in": conv_w_in, "conv_w": conv_w, "conv_w_out": conv_w_out}


def reference(x, c, ada_w, ada_b, norm_g, ffn_w_in, ffn_w_out, pool_w, pool_scale,
              fox_w_in, fox_b_f, fox_q_gain, fox_k_gain, fox_w_o,
              s5_lam_re, s5_lam_im, s5_log_dt, s5_b_re, s5_b_im, s5_c_re, s5_c_im,
              s5_d, s5_w_glu, conv_w_in, conv_w, conv_w_out):
    b_ = x.shape[0]
    cond = jax.nn.silu(c)
    for i in range(DEPTH):
        mod = (cond @ ada_w[i] + ada_b[i]).reshape(b_, N_SUBLAYERS, 3, D_MODEL)
        h = adaln(x, norm_g[i, 0], mod[:, 0, 0], mod[:, 0, 1])
        x = x + 0.5 * (1.0 + mod[:, 0, 2][:, None, :]) * swiglu(h, ffn_w_in[i, 0], ffn_w_out[i, 0])
        h = adaln(x, norm_g[i, 1], mod[:, 1, 0], mod[:, 1, 1])
        m, r = i % N_MIXERS, i // N_MIXERS
        if m == 0:
            y = pool_mixer(h, pool_w[r], pool_scale[r])
        elif m == 1:
            y = fox_attention(h, fox_w_in[r], fox_b_f[r], fox_q_gain[r], fox_k_gain[r], fox_w_o[r])
        elif m == 2:
            y = s5_mixer(h, s5_lam_re[r], s5_lam_im[r], s5_log_dt[r], s5_b_re[r], s5_b_im[r],
                         s5_c_re[r], s5_c_im[r], s5_d[r], s5_w_glu[r])
        else:
            y = short_conv_mixer(h, conv_w_in[r], conv_w[r], conv_w_out[r])
        x = x + (1.0 + mod[:, 1, 2][:, None, :]) * y
        h = adaln(x, norm_g[i, 2], mod[:, 2, 0], mod[:, 2, 1])
        x = x + 0.5 * (1.0 + mod[:, 2, 2][:, None, :]) * swiglu(h, ffn_w_in[i, 1], ffn_w_out[i, 1])
    return x
```

```python
import math
import numpy as np
from contextlib import ExitStack
import concourse.bass as bass
import concourse.mybir as mybir
from concourse.bass_utils import run_bass_kernel_spmd

F32 = mybir.dt.float32
BF16 = mybir.dt.bfloat16
I32 = mybir.dt.int32
AF = mybir.ActivationFunctionType
ALU = mybir.AluOpType

D = 1024
KC = 8
DFF = 2816
MC = 22
NG = 11
EPS = 1e-6
NCORES = 8


class Buf:
    __slots__ = ("name", "last_w", "rc", "rd")

    def __init__(self, S=None, name=""):
        self.name = name
        self.last_w = None
        self.rc = {}
        self.rd = []
        if S is not None:
            S.bufs.append(self)

    def reader_toks(self):
        return list(self.rc.values()) + self.rd

    def add_reader(self, tok):
        if tok[0] == 'c':
            self.rc[tok[1]] = tok
        else:
            self.rd.append(tok)
            if len(self.rd) > 24:
                self.rd = self.rd[-24:]

    def clear_readers(self):
        self.rc = {}
        self.rd = []


class Sched:
    NS = 8

    def __init__(self, nc):
        self.nc = nc
        self.names = ["pe", "dve", "act", "pool", "sp"]
        self.ops = {k: [] for k in self.names}
        self.cw = {}
        self.dw = {}
        self.kw = set()
        self.ndma = {k: 0 for k in self.names}
        self.targets = {k: set() for k in self.names}
        self.ncoll = 0
        self.bufs = []
        self.same_engine_sync = True

    def _need(self, eng, tok):
        if tok is None:
            return False
        if tok[0] == 'c':
            _, te, idx = tok
            if te == eng and (eng == 'pe' or not self.same_engine_sync):
                return False
            key = (eng, te)
            if self.cw.get(key, -1) >= idx:
                return False
            self.cw[key] = idx
            self.targets[te].add(idx)
            return True
        elif tok[0] == 'k':
            if (eng, tok[2]) in self.kw:
                return False
            self.kw.add((eng, tok[2]))
            return True
        else:
            _, q, di = tok
            key = (eng, q, di % self.NS)
            if self.dw.get(key, -1) >= di:
                return False
            self.dw[key] = di
            return True

    def op(self, eng, fn, reads=(), writes=(), dma=False, coll=False):
        deps = []
        for b in reads:
            if b.last_w is not None:
                deps.append(b.last_w)
        for b in writes:
            if b.last_w is not None:
                deps.append(b.last_w)
            deps.extend(b.reader_toks())
        waits = []
        seen = set()
        for t in deps:
            if t in seen:
                continue
            seen.add(t)
            if self._need(eng, t):
                waits.append(t)
        idx = len(self.ops[eng])
        if coll:
            tok = ('k', eng, self.ncoll)
            self.ncoll += 1
        elif dma:
            di = self.ndma[eng]
            self.ndma[eng] += 1
            tok = ('d', eng, di)
            if di >= self.NS:
                prev = ('d', eng, di - self.NS)
                if self._need(eng, prev):
                    waits.append(prev)
        else:
            tok = ('c', eng, idx)
        self.ops[eng].append((fn, waits, tok))
        for b in reads:
            b.add_reader(tok)
        for b in writes:
            b.last_w = tok
            b.clear_readers()
        return tok

    def barrier(self):
        toks = []
        seen = set()
        for b in self.bufs:
            for t in ([b.last_w] if b.last_w is not None else []) + b.reader_toks():
                if t not in seen:
                    seen.add(t)
                    toks.append(t)
            b.last_w = None
            b.clear_readers()
        toks.sort(key=lambda t: (t[0], t[1], -t[2]))
        for e in self.names:
            waits = [t for t in toks if self._need(e, t)]
            self.ops[e].append((None, waits, None))

    def emit(self, stack):
        nc = self.nc
        csem = {k: stack.enter_context(nc.semaphore("c_" + k)) for k in self.names}
        dsem = {}
        for k in self.names:
            if self.ndma[k] > 0:
                dsem[k] = [stack.enter_context(nc.semaphore("d_%s_%d" % (k, s))) for s in range(self.NS)]
        ksem = [stack.enter_context(nc.semaphore("k_%d" % i)) for i in range(self.ncoll)]
        cval = {}
        for k in self.names:
            n = 0
            for idx in sorted(self.targets[k]):
                n += 1
                cval[(k, idx)] = n

        def tokval(tok):
            if tok[0] == 'c':
                return csem[tok[1]], cval[(tok[1], tok[2])]
            if tok[0] == 'k':
                return ksem[tok[2]], 1
            _, q, di = tok
            return dsem[q][di % self.NS], 16 * (di // self.NS + 1)

        block = stack.enter_context(nc.Block())

        def make(k):
            def body(e):
                for (fn, waits, tok) in self.ops[k]:
                    for w in waits:
                        s, v = tokval(w)
                        e.wait_ge(s, v)
                    if fn is None:
                        continue
                    ins = fn(e)
                    if tok[0] == 'd':
                        s, v = tokval(tok)
                        ins.then_inc(s, 16)
                    elif tok[0] == 'k':
                        s, v = tokval(tok)
                        ins.then_inc(s)
                    elif tok[2] in self.targets[k]:
                        ins.then_inc(csem[k], 1)
            return body

        block.tensor(make("pe"))
        block.vector(make("dve"))
        block.scalar(make("act"))
        block.gpsimd(make("pool"))
        block.sync(make("sp"))


class Arena:
    def __init__(self, t, nwords):
        self.t = t
        self.n = nwords
        self.off = 0

    def reset(self):
        self.off = 0

    def alloc(self, shape, dt, parts=None):
        n = 1
        for s in shape[1:]:
            n *= s
        words = n if dt in (F32, I32) else (n + 1) // 2
        words = (words + 1) // 2 * 2
        P = shape[0]
        ap = self.t[0:P, self.off:self.off + words]
        self.off += words
        assert self.off <= self.n, ("arena overflow", self.off, self.n)
        if dt != F32:
            ap = ap.bitcast(dt)
        ap = ap[:, 0:n]
        if len(shape) == 3:
            ap = ap.rearrange("p (a b) -> p a b", a=shape[1])
        elif len(shape) == 4:
            ap = ap.rearrange("p (a b c) -> p a b c", a=shape[1], b=shape[2])
        return ap


class Ring:
    def __init__(self, S, aps):
        self.items = [(a, Buf(S)) for a in aps]
        self.i = 0

    def next(self):
        it = self.items[self.i % len(self.items)]
        self.i += 1
        return it


DEBUG = False
FOX_STAGE = 3
FOX_DBG = ''


def build(SEQ, mixers=(0, 1, 2, 3), nlayers=4, do_ffn=True):
    nc = bass.Bass("TRN2", target_bir_lowering=False)
    S = Sched(nc)

    def din(name, shape, dt=F32):
        return nc.dram_tensor(name, list(shape), dt, kind="ExternalInput").ap()

    HALF = SEQ // 2
    MIXC = 3088 + 1024 + 1024 + 3072 + 1024 + 256
    xTh_in = din("xTh", [KC * 128, HALF])
    cTall_in = din("cTall", [128, KC, 4])
    bsel_in = din("bsel", [128, 4])
    ada_w_s = din("ada_w_s", [4, D, 1152])
    ada_bT = din("ada_bT", [128, 288])
    norm_gT = din("norm_gT", [128, 96])
    win_s = din("win_s", [D, 2 * DFF])
    wout_s = din("wout_s", [DFF, D])
    mixw_s = din("mixw_s", [128, MIXC])
    pool_sT = din("pool_sT", [128, KC])
    pool_icnt = din("pool_icnt", [128, KC, 16])
    fox_bf = din("fox_bf", [16, 1])
    fox_qg = din("fox_qg", [128, 1])
    fox_kg = din("fox_kg", [128, 1])
    s5_lre = din("s5_lre", [128, 32])
    s5_lim = din("s5_lim", [128, 32])
    s5_ldt = din("s5_ldt", [128, 32])
    s5_bre = din("s5_bre", [128, 32, 16])
    s5_bim = din("s5_bim", [128, 32, 16])
    s5_cre = din("s5_cre", [128, 32, 16])
    s5_cim = din("s5_cim", [128, 32, 16])
    s5_dT = din("s5_dT", [128, KC])
    conv_wT = din("conv_wT", [128, 24])
    ident_in = din("ident", [128, 128])
    tri_in = din("tri", [128, 128])
    bones_in = din("bones", [128, 128])
    iota_in = din("iota", [128, 256])
    outT = nc.dram_tensor("outT", [KC, 2, 128, HALF], F32, kind="ExternalOutput").ap()
    dbg = nc.dram_tensor("dbg", [128, 4096], F32, kind="ExternalOutput").ap() if DEBUG else None

    def dscr(name, shape, dt):
        return nc.dram_tensor(name, list(shape), dt).ap()

    ALL8 = [list(range(8))]
    PAIRS = [[0, 1], [2, 3], [4, 5], [6, 7]]
    x_bounce = dscr("x_bounce", [KC * 128, HALF], F32)
    xs2d = dscr("xs", [2 * KC * 128, HALF], F32)
    xs = xs2d.rearrange("(k r p) t -> k r p t", r=2, k=KC)
    win_bo = dscr("win_bo", [2 * NG * 128, KC * 256], BF16)
    win_all = dscr("win_all", [8 * 2 * NG * 128, KC * 256], BF16)
    win_bf = win_all.rearrange("(r g p) (k c) -> r g p k c", r=8, g=2 * NG, k=KC)
    wout_bo = dscr("wout_bo", [KC * 128, MC * 128], BF16)
    wout_all = dscr("wout_all", [8 * KC * 128, MC * 128], BF16)
    wout_bf = wout_all.rearrange("(r n p) (m j) -> r n p m j", r=8, n=KC, m=MC)
    mixw_bo = dscr("mixw_bo", [128, MIXC], BF16)
    mixw_all = dscr("mixw_all", [D, MIXC], BF16)
    foxin_bf = mixw_all[:, 0:3088]
    foxo_bf = mixw_all[:, 3088:4112]
    glu_bf = mixw_all[:, 4112:5136]
    convin_bf = mixw_all[:, 5136:8208]
    convo_bf = mixw_all[:, 8208:9232]
    poolw_bf = mixw_all[:, 9232:9488].rearrange("(g a) b -> g a b", g=4)
    mod_bo = dscr("mod_bo", [128, 144], F32)
    mod_all = dscr("mod_all", [8 * 128, 144], F32)
    NKB = SEQ // 128
    NQB = SEQ // 512
    def dscr2(name, shape, dt):
        if DEBUG:
            return nc.dram_tensor(name, list(shape), dt, kind="ExternalOutput").ap()
        return dscr(name, shape, dt)

    qTa = dscr("qTa", [16, 66, SEQ], BF16)
    kTa = dscr("kTa", [16, 66, SEQ], BF16)
    vS = dscr("vS", [16, NKB, 128, 64], BF16)
    oTs = dscr("oTs", [16, 64, SEQ], BF16)
    cD = dscr("cD", [16, NQB], F32)

    b_xs = [Buf(S, "xs%d" % i) for i in range(max(1, SEQ // 256))]

    def xs_bufs(t0, n):
        return b_xs[t0 // 256:(t0 + n + 255) // 256]

    def xview(base, t0, n):
        r, tt = t0 // HALF, t0 % HALF
        return base[:, r, :, tt:tt + n].rearrange("k p t -> p k t")

    with ExitStack() as st:
        arena_t = st.enter_context(nc.sbuf_tensor("arena", [128, 46 * 1024], F32))
        AR = Arena(arena_t, 46 * 1024)
        cst_t = st.enter_context(nc.sbuf_tensor("cst", [128, 2048], F32))
        CST = Arena(cst_t, 2048)
        psum = [st.enter_context(nc.psum_tensor("ps%d" % i, [128, 512], F32)) for i in range(8)]
        PB = [Buf(S, "psum%d" % i) for i in range(8)]

        ident = CST.alloc([128, 128], F32)
        ones_bf = CST.alloc([128, 128], BF16)
        bones_bf = CST.alloc([128, 128], BF16)
        tri_bf = CST.alloc([128, 128], BF16)
        ones_f = CST.alloc([128, 64], F32)
        modT = CST.alloc([128, 288], F32)
        SPt = CST.alloc([128, 96], F32)
        GPt = CST.alloc([128, 96], F32)
        ngT = CST.alloc([128, 96], F32)
        condA = CST.alloc([128, KC, 4], F32)
        bsel = CST.alloc([128, 4], F32)
        b_c = Buf(S, "consts")
        b_mod = Buf(S, "mod")

        S.op("sp", lambda e: e.dma_start(out=ident, in_=ident_in), writes=[b_c], dma=True)
        S.op("sp", lambda e: e.dma_start(out=ngT, in_=norm_gT), writes=[b_c], dma=True)
        S.op("sp", lambda e: e.dma_start(out=condA, in_=cTall_in), writes=[b_c], dma=True)
        S.op("sp", lambda e: e.dma_start(out=bsel, in_=bsel_in), writes=[b_c], dma=True)
        S.op("dve", lambda e: e.memset(ones_bf, 1.0), writes=[b_c])
        S.op("dve", lambda e: e.memset(ones_f, 1.0), writes=[b_c])
        tmpc = AR.alloc([128, 128], F32)
        tmpc2 = AR.alloc([128, 128], F32)
        b_t = Buf(S)
        S.op("sp", lambda e: e.dma_start(out=tmpc, in_=tri_in), writes=[b_t], dma=True)
        S.op("dve", lambda e: e.tensor_copy(out=tri_bf, in_=tmpc), reads=[b_t], writes=[b_c])
        S.op("sp", lambda e: e.dma_start(out=tmpc2, in_=bones_in), writes=[b_t], dma=True)
        S.op("dve", lambda e: e.tensor_copy(out=bones_bf, in_=tmpc2), reads=[b_t], writes=[b_c])
        S.op("act", lambda e: e.activation(out=condA, in_=condA, func=AF.Silu), reads=[b_c], writes=[b_c])

        b_xbo = Buf(S, "xbo")
        S.op("sp", lambda e: e.dma_start(out=x_bounce, in_=xTh_in), writes=[b_xbo], dma=True)
        for k in range(KC):
            S.op("pool", lambda e, k=k: e.collective_compute("AllGather", ALU.bypass, replica_groups=PAIRS, ins=[x_bounce[k * 128:(k + 1) * 128, :]],
                                                             outs=[xs2d[k * 256:(k + 1) * 256, :]]), reads=[b_xbo], writes=b_xs, coll=True)

        b_wbo = Buf(S, "wbo")
        b_win = Buf(S, "win_all")
        b_wout = Buf(S, "wout_all")
        b_mixw_all = Buf(S, "mixw_all")
        win_bo_v = win_bo.rearrange("(g p) (k c) -> g p k c", g=2 * NG, k=KC)
        if do_ffn:
            for g in range(2 * NG):
                src = win_s[:, g * 256:(g + 1) * 256].rearrange("(k p) c -> p k c", p=128)
                S.op("pool", lambda e, g=g, src=src: e.dma_start(out=win_bo_v[g], in_=src), writes=[b_wbo], dma=True)
            S.op("pool", lambda e: e.collective_compute("AllGather", ALU.bypass, replica_groups=ALL8, ins=[win_bo], outs=[win_all]),
                 reads=[b_wbo], writes=[b_win], coll=True)
            wout_bo_v = wout_bo.rearrange("(n p) (m j) -> n p m j", n=KC, m=MC)
            b_wbo2 = Buf(S, "wbo2")
            for n in range(KC):
                src = wout_s[:, n * 128:(n + 1) * 128].rearrange("(m p) j -> p m j", p=128)
                S.op("pool", lambda e, n=n, src=src: e.dma_start(out=wout_bo_v[n], in_=src), writes=[b_wbo2], dma=True)
            S.op("pool", lambda e: e.collective_compute("AllGather", ALU.bypass, replica_groups=ALL8, ins=[wout_bo], outs=[wout_all]),
                 reads=[b_wbo2], writes=[b_wout], coll=True)
        b_wbo3 = Buf(S, "wbo3")
        S.op("pool", lambda e: e.dma_start(out=mixw_bo, in_=mixw_s), writes=[b_wbo3], dma=True)
        S.op("pool", lambda e: e.collective_compute("AllGather", ALU.bypass, replica_groups=ALL8, ins=[mixw_bo], outs=[mixw_all]),
             reads=[b_wbo3], writes=[b_mixw_all], coll=True)
        b_mixw = {k: b_mixw_all for k in ["pool", "foxin", "foxo", "glu", "convin", "convo"]}

        S.barrier()
        AR.reset()
        ada_ring = Ring(S, [AR.alloc([128, KC, 128], F32) for _ in range(4)])
        pmod = psum[7]
        for i in range(nlayers):
            for jj in range(9):
                at, ab = ada_ring.next()
                src = ada_w_s[i][:, jj * 128:(jj + 1) * 128].rearrange("(k p) c -> p k c", p=128)
                S.op("sp", lambda e, at=at, src=src: e.dma_start(out=at, in_=src), writes=[ab], dma=True)
                col = (i * 9 + jj) * 4
                for k in range(KC):
                    S.op("pe", lambda e, at=at, k=k, col=col: e.matmul(
                        pmod[:, col:col + 4], lhsT=at[:, k, :], rhs=condA[:, k, :],
                        start=(k == 0), stop=(k == KC - 1)), reads=[ab, b_c], writes=[PB[7]])
        mpart = AR.alloc([128, 144], F32)
        b_mp = Buf(S)
        if nlayers < 4:
            S.op("dve", lambda e: e.memset(mpart, 0.0), writes=[b_mp])
        S.op("dve", lambda e: e.tensor_copy(out=mpart[:, 0:nlayers * 36], in_=pmod[:, 0:nlayers * 36]), reads=[PB[7]], writes=[b_mp])
        b_mbo = Buf(S)
        b_mall = Buf(S)
        S.op("sp", lambda e: e.dma_start(out=mod_bo, in_=mpart), reads=[b_mp], writes=[b_mbo], dma=True)
        S.op("pool", lambda e: e.collective_compute("AllGather", ALU.bypass, replica_groups=ALL8, ins=[mod_bo], outs=[mod_all]),
             reads=[b_mbo], writes=[b_mall], coll=True)
        G = AR.alloc([128, 8, 144], F32)
        b_G = Buf(S)
        S.op("sp", lambda e: e.dma_start(out=G, in_=mod_all.rearrange("(r p) c -> p r c", p=128)), reads=[b_mall], writes=[b_G], dma=True)
        abT = AR.alloc([128, 288], F32)
        b_ab = Buf(S)
        S.op("sp", lambda e: e.dma_start(out=abT, in_=ada_bT), writes=[b_ab], dma=True)
        G5 = G.rearrange("p r (i j b) -> p r i j b", i=4, j=9)
        for i in range(4):
            mo = modT[:, i * 72:(i + 1) * 72].rearrange("p (r j) -> p r j", r=8)
            for b in range(4):
                gv = G5[:, :, i, :, b]
                if b == 0:
                    S.op("dve", lambda e, mo=mo, gv=gv, b=b: e.tensor_scalar(out=mo, in0=gv, scalar1=bsel[:, b:b + 1], scalar2=None, op0=ALU.mult),
                         reads=[b_G, b_c], writes=[b_mod])
                else:
                    S.op("dve", lambda e, mo=mo, gv=gv, b=b: e.scalar_tensor_tensor(out=mo, in0=gv, scalar=bsel[:, b:b + 1], in1=mo,
                                                                                 op0=ALU.mult, op1=ALU.add), reads=[b_G, b_c, b_mod], writes=[b_mod])
        S.op("dve", lambda e: e.tensor_tensor(out=modT, in0=modT, in1=abT, op=ALU.add), reads=[b_mod, b_ab], writes=[b_mod])
        for i in range(nlayers):
            for s in range(3):
                c0 = (i * 3 + s) * 8
                sc = i * 72 + (s * 3 + 1) * 8
                gt = i * 72 + (s * 3 + 2) * 8
                S.op("dve", lambda e, c0=c0, sc=sc: e.scalar_tensor_tensor(
                    out=SPt[:, c0:c0 + 8], in0=modT[:, sc:sc + 8], scalar=1.0, in1=ngT[:, c0:c0 + 8],
                    op0=ALU.add, op1=ALU.mult), reads=[b_mod, b_c], writes=[b_mod])
                half = 1.0 if s == 1 else 0.5
                S.op("dve", lambda e, c0=c0, gt=gt, half=half: e.tensor_scalar(
                    out=GPt[:, c0:c0 + 8], in0=modT[:, gt:gt + 8], scalar1=1.0, scalar2=half,
                    op0=ALU.add, op1=ALU.mult), reads=[b_mod], writes=[b_mod])

        def shift_col(i, s, k):
            return modT[:, i * 72 + (s * 3) * 8 + k: i * 72 + (s * 3) * 8 + k + 1]

        def sp_col(i, s, k):
            c = (i * 3 + s) * 8 + k
            return SPt[:, c:c + 1]

        def gp_col(i, s, k):
            c = (i * 3 + s) * 8 + k
            return GPt[:, c:c + 1]

        def prenorm(i, s, xb, bx, W, sq, bsq, rs, brs, tmp_ring, outs):
            pss, bps = psum[6], PB[6]
            for k in range(KC):
                S.op("pool", lambda e, k=k: e.tensor_tensor(out=sq[:, k, 0:W], in0=xb[:, k, 0:W], in1=xb[:, k, 0:W], op=ALU.mult),
                     reads=[bx], writes=[bsq])
            for k in range(KC):
                S.op("pe", lambda e, k=k: e.matmul(pss[:, 0:W], lhsT=ones_bf, rhs=sq[:, k, 0:W], start=(k == 0), stop=(k == KC - 1)),
                     reads=[bsq, b_c], writes=[bps])
            S.op("act", lambda e: e.activation(out=rs[:, 0:W], in_=pss[:, 0:W], func=AF.Sqrt, bias=EPS, scale=1.0 / D),
                 reads=[bps], writes=[brs])
            S.op("dve", lambda e: e.reciprocal(out=rs[:, 0:W], in_=rs[:, 0:W]), reads=[brs], writes=[brs])
            for k in range(KC):
                tt, tb = tmp_ring.next()
                S.op("dve", lambda e, k=k, tt=tt: e.tensor_tensor(out=tt[:, 0:W], in0=xb[:, k, 0:W], in1=rs[:, 0:W], op=ALU.mult),
                     reads=[bx, brs], writes=[tb])
                for (o, bo) in outs:
                    S.op("act", lambda e, k=k, tt=tt, o=o: e.activation(
                        out=o(k), in_=tt[:, 0:W], func=AF.Identity, bias=shift_col(i, s, k), scale=sp_col(i, s, k)),
                        reads=[tb, b_mod], writes=[bo])

        cur_src = [xs]

        def x_src():
            return cur_src[0]

        def ffn_phase(i, f, s, dst):
            S.barrier()
            AR.reset()
            TB = 1024 if HALF % 1024 == 0 else 512
            NSB = TB // 512
            src = x_src()
            xb_ring = Ring(S, [AR.alloc([128, KC, TB], F32) for _ in range(1)])
            h = AR.alloc([128, KC, TB], BF16)
            bh = Buf(S)
            hid = AR.alloc([128, MC, TB], BF16)
            bhid = Buf(S)
            sq = AR.alloc([128, KC, 512], BF16)
            bsq = Buf(S)
            rs = AR.alloc([128, 512], F32)
            brs = Buf(S)
            tmp_ring = Ring(S, [AR.alloc([128, 512], F32) for _ in range(2)])
            sg_ring = Ring(S, [AR.alloc([128, 512], F32) for _ in range(2)])
            wg_ring = Ring(S, [AR.alloc([128, KC, 256], BF16) for _ in range(2)])
            wu_ring = Ring(S, [AR.alloc([128, KC, 256], BF16) for _ in range(2)])
            wo_ring = Ring(S, [AR.alloc([128, MC, 128], BF16) for _ in range(2)])
            pg_ring = Ring(S, [])
            pg_ring.items = [(psum[0], PB[0]), (psum[1], PB[1])]
            pu_ring = Ring(S, [])
            pu_ring.items = [(psum[2], PB[2]), (psum[3], PB[3])]
            py_ring = Ring(S, [])
            py_ring.items = [(psum[4], PB[4]), (psum[5], PB[5])]
            bwi = b_win
            bwo = b_wout
            fi = i * 2 + f
            for t0 in range(0, SEQ, TB):
                xb, bx = xb_ring.next()
                S.op("sp", lambda e, xb=xb, t0=t0: e.dma_start(out=xb, in_=xview(src, t0, TB)),
                     reads=xs_bufs(t0, TB), writes=[bx], dma=True)
                for sb in range(NSB):
                    c0 = sb * 512
                    xv = xb[:, :, c0:c0 + 512]
                    prenorm(i, s, xv, bx, 512, sq, bsq, rs, brs, tmp_ring,
                            [(lambda k, c0=c0: h[:, k, c0:c0 + 512], bh)])
                for g in range(NG):
                    wg, bwg = wg_ring.next()
                    wu, bwu = wu_ring.next()
                    S.op("sp", lambda e, wg=wg, g=g: e.dma_start(out=wg, in_=win_bf[fi, g]), reads=[bwi], writes=[bwg], dma=True)
                    S.op("sp", lambda e, wu=wu, g=g: e.dma_start(out=wu, in_=win_bf[fi, NG + g]), reads=[bwi], writes=[bwu], dma=True)
                    for mm in range(2):
                        m = g * 2 + mm
                        for sb in range(NSB):
                            c0 = sb * 512
                            pg, bpg = pg_ring.next()
                            pu, bpu = pu_ring.next()
                            for k in range(KC):
                                S.op("pe", lambda e, pg=pg, wg=wg, mm=mm, k=k, c0=c0: e.matmul(
                                    pg[:, :], lhsT=wg[:, k, mm * 128:(mm + 1) * 128], rhs=h[:, k, c0:c0 + 512],
                                    start=(k == 0), stop=(k == KC - 1)), reads=[bwg, bh], writes=[bpg])
                            for k in range(KC):
                                S.op("pe", lambda e, pu=pu, wu=wu, mm=mm, k=k, c0=c0: e.matmul(
                                    pu[:, :], lhsT=wu[:, k, mm * 128:(mm + 1) * 128], rhs=h[:, k, c0:c0 + 512],
                                    start=(k == 0), stop=(k == KC - 1)), reads=[bwu, bh], writes=[bpu])
                            sg, bsg = sg_ring.next()
                            S.op("act", lambda e, sg=sg, pg=pg: e.activation(out=sg, in_=pg[:, :], func=AF.Silu), reads=[bpg], writes=[bsg])
                            S.op("dve", lambda e, sg=sg, pu=pu, m=m, c0=c0: e.tensor_tensor(
                                out=hid[:, m, c0:c0 + 512], in0=sg, in1=pu[:, :], op=ALU.mult), reads=[bsg, bpu], writes=[bhid])
                for n in range(KC):
                    wo, bwo_t = wo_ring.next()
                    S.op("sp", lambda e, wo=wo, n=n: e.dma_start(out=wo, in_=wout_bf[fi, n]), reads=[bwo], writes=[bwo_t], dma=True)
                    for sb in range(NSB):
                        c0 = sb * 512
                        py, bpy = py_ring.next()
                        for m in range(MC):
                            S.op("pe", lambda e, py=py, wo=wo, m=m, c0=c0: e.matmul(
                                py[:, :], lhsT=wo[:, m, :], rhs=hid[:, m, c0:c0 + 512], start=(m == 0), stop=(m == MC - 1)),
                                reads=[bwo_t, bhid], writes=[bpy])
                        S.op("dve", lambda e, py=py, n=n, c0=c0, xb=xb: e.scalar_tensor_tensor(
                            out=xb[:, n, c0:c0 + 512], in0=py[:, :], scalar=gp_col(i, s, n), in1=xb[:, n, c0:c0 + 512],
                            op0=ALU.mult, op1=ALU.add), reads=[bpy, b_mod, bx], writes=[bx])
                S.op("sp", lambda e, xb=xb, t0=t0: e.dma_start(out=xview(dst, t0, TB), in_=xb),
                     reads=[bx], writes=xs_bufs(t0, TB), dma=True)
            cur_src[0] = dst

        def out_proj_residual(i, wo_sb, bwo_sb, rhs_fn, brhs, xb, bx, W, py_ring):
            for n in range(KC):
                py, bpy = py_ring.next()
                for k in range(KC):
                    S.op("pe", lambda e, py=py, n=n, k=k: e.matmul(
                        py[:, 0:W], lhsT=wo_sb[:, k, n * 128:(n + 1) * 128], rhs=rhs_fn(k), start=(k == 0), stop=(k == KC - 1)),
                        reads=[bwo_sb, brhs], writes=[bpy])
                S.op("dve", lambda e, py=py, n=n: e.scalar_tensor_tensor(
                    out=xb[:, n, 0:W], in0=py[:, 0:W], scalar=gp_col(i, 1, n), in1=xb[:, n, 0:W],
                    op0=ALU.mult, op1=ALU.add), reads=[bpy, b_mod, bx], writes=[bx])

        def conv_phase(i, dst):
            S.barrier()
            AR.reset()
            W = 512
            src = x_src()
            xb = AR.alloc([128, KC, W], F32)
            bx = Buf(S)
            h = AR.alloc([128, KC, W], BF16)
            bh = Buf(S)
            sq = AR.alloc([128, KC, 512], BF16)
            bsq = Buf(S)
            rs = AR.alloc([128, 512], F32)
            brs = Buf(S)
            tmp_ring = Ring(S, [AR.alloc([128, 512], F32) for _ in range(2)])
            cz = AR.alloc([128, KC, W + 2], F32)
            bcz = Buf(S)
            csb_ring = Ring(S, [AR.alloc([128, W], F32) for _ in range(2)])
            t1_ring = Ring(S, [AR.alloc([128, W], F32) for _ in range(2)])
            mmb = AR.alloc([128, KC, W], BF16)
            bmm = Buf(S)
            win = AR.alloc([128, KC, 3 * D], BF16)
            bwin_sb = Buf(S)
            wo = AR.alloc([128, KC, D], BF16)
            bwo_sb = Buf(S)
            cw = AR.alloc([128, 24], F32)
            bcw = Buf(S)
            S.op("sp", lambda e: e.dma_start(out=cw, in_=conv_wT), writes=[bcw], dma=True)
            for k in range(KC):
                S.op("sp", lambda e, k=k: e.dma_start(out=win[:, k, :], in_=convin_bf[k * 128:(k + 1) * 128, :]),
                     reads=[b_mixw["convin"]], writes=[bwin_sb], dma=True)
            S.op("sp", lambda e: e.dma_start(out=wo, in_=convo_bf.rearrange("(k p) n -> p k n", p=128)),
                 reads=[b_mixw["convo"]], writes=[bwo_sb], dma=True)
            S.op("dve", lambda e: e.memset(cz, 0.0), writes=[bcz])
            pa_ring = Ring(S, [])
            pa_ring.items = [(psum[0], PB[0]), (psum[1], PB[1])]
            pb_ring = Ring(S, [])
            pb_ring.items = [(psum[2], PB[2]), (psum[3], PB[3])]
            py_ring = Ring(S, [])
            py_ring.items = [(psum[4], PB[4]), (psum[5], PB[5])]
            for t0 in range(0, SEQ, W):
                S.op("sp", lambda e, t0=t0: e.dma_start(out=xb, in_=xview(src, t0, W)),
                     reads=xs_bufs(t0, W), writes=[bx], dma=True)
                prenorm(i, 1, xb, bx, W, sq, bsq, rs, brs, tmp_ring, [(lambda k: h[:, k, :], bh)])
                for k in range(KC):
                    pc, bpc = pa_ring.next()
                    pz, bpz = pb_ring.next()
                    for kk in range(KC):
                        S.op("pe", lambda e, pc=pc, k=k, kk=kk: e.matmul(
                            pc[:, 0:W], lhsT=win[:, kk, D + k * 128:D + (k + 1) * 128], rhs=h[:, kk, :],
                            start=(kk == 0), stop=(kk == KC - 1)), reads=[bwin_sb, bh], writes=[bpc])
                    for kk in range(KC):
                        S.op("pe", lambda e, pz=pz, k=k, kk=kk: e.matmul(
                            pz[:, 0:W], lhsT=win[:, kk, 2 * D + k * 128:2 * D + (k + 1) * 128], rhs=h[:, kk, :],
                            start=(kk == 0), stop=(kk == KC - 1)), reads=[bwin_sb, bh], writes=[bpz])
                    cs, bcs = csb_ring.next()
                    S.op("act", lambda e, cs=cs, pc=pc: e.activation(out=cs, in_=pc[:, 0:W], func=AF.Identity), reads=[bpc], writes=[bcs])
                    S.op("dve", lambda e, cs=cs, pz=pz, k=k: e.tensor_tensor(out=cz[:, k, 2:2 + W], in0=cs, in1=pz[:, 0:W], op=ALU.mult),
                         reads=[bcs, bpz], writes=[bcz])
                    t1, bt1 = t1_ring.next()
                    S.op("dve", lambda e, t1=t1, k=k: e.tensor_scalar(out=t1, in0=cz[:, k, 2:2 + W], scalar1=cw[:, 16 + k:17 + k], scalar2=None,
                                                                      op0=ALU.mult), reads=[bcz, bcw], writes=[bt1])
                    S.op("dve", lambda e, t1=t1, k=k: e.scalar_tensor_tensor(out=t1, in0=cz[:, k, 1:1 + W], scalar=cw[:, 8 + k:9 + k], in1=t1,
                                                                             op0=ALU.mult, op1=ALU.add), reads=[bcz, bcw, bt1], writes=[bt1])
                    S.op("dve", lambda e, t1=t1, k=k: e.scalar_tensor_tensor(out=t1, in0=cz[:, k, 0:W], scalar=cw[:, k:k + 1], in1=t1,
                                                                             op0=ALU.mult, op1=ALU.add), reads=[bcz, bcw, bt1], writes=[bt1])
                    pbg, bpbg = pa_ring.next()
                    for kk in range(KC):
                        S.op("pe", lambda e, pbg=pbg, k=k, kk=kk: e.matmul(
                            pbg[:, 0:W], lhsT=win[:, kk, k * 128:(k + 1) * 128], rhs=h[:, kk, :],
                            start=(kk == 0), stop=(kk == KC - 1)), reads=[bwin_sb, bh], writes=[bpbg])
                    S.op("dve", lambda e, t1=t1, pbg=pbg, k=k: e.tensor_tensor(out=mmb[:, k, :], in0=t1, in1=pbg[:, 0:W], op=ALU.mult),
                         reads=[bt1, bpbg], writes=[bmm])
                    S.op("pool", lambda e, k=k: e.tensor_copy(out=cz[:, k, 0:2], in_=cz[:, k, W:W + 2]), reads=[bcz], writes=[bcz])
                out_proj_residual(i, wo, bwo_sb, lambda k: mmb[:, k, :], bmm, xb, bx, W, py_ring)
                S.op("sp", lambda e, t0=t0: e.dma_start(out=xview(dst, t0, W), in_=xb),
                     reads=[bx], writes=xs_bufs(t0, W), dma=True)
            cur_src[0] = dst

        def pool_phase(i, dst):
            S.barrier()
            AR.reset()
            W = 512
            src = x_src()
            xb = AR.alloc([128, KC, W], F32)
            bx = Buf(S)
            hf = AR.alloc([128, KC, W + 16], F32)
            bhf = Buf(S)
            sq = AR.alloc([128, KC, 512], BF16)
            bsq = Buf(S)
            rs = AR.alloc([128, 512], F32)
            brs = Buf(S)
            tmp_ring = Ring(S, [AR.alloc([128, 512], F32) for _ in range(2)])
            sa_ring = Ring(S, [AR.alloc([128, W + 16], F32) for _ in range(2)])
            sb_ring = Ring(S, [AR.alloc([128, W + 16], F32) for _ in range(2)])
            pl = AR.alloc([128, KC, W], BF16)
            bpl = Buf(S)
            pw = AR.alloc([128, 4, 2, 256], BF16)
            bpw = Buf(S)
            icnt = AR.alloc([128, KC, 16], F32)
            psT = AR.alloc([128, KC], F32)
            gps = AR.alloc([128, KC], F32)
            bcn = Buf(S)
            S.op("sp", lambda e: e.dma_start(out=icnt, in_=pool_icnt), writes=[bcn], dma=True)
            S.op("sp", lambda e: e.dma_start(out=psT, in_=pool_sT), writes=[bcn], dma=True)
            c0g = (i * 3 + 1) * 8
            S.op("dve", lambda e: e.tensor_tensor(out=gps, in0=psT, in1=GPt[:, c0g:c0g + 8], op=ALU.mult), reads=[bcn, b_mod], writes=[bcn])
            S.op("sp", lambda e: e.dma_start(out=pw, in_=poolw_bf.rearrange("g (c p) d -> p g c d", p=128)),
                 reads=[b_mixw["pool"]], writes=[bpw], dma=True)
            S.op("dve", lambda e: e.memset(hf, 0.0), writes=[bhf])
            py_ring = Ring(S, [])
            py_ring.items = [(psum[4], PB[4]), (psum[5], PB[5])]
            for t0 in range(0, SEQ, W):
                S.op("sp", lambda e, t0=t0: e.dma_start(out=xb, in_=xview(src, t0, W)),
                     reads=xs_bufs(t0, W), writes=[bx], dma=True)
                prenorm(i, 1, xb, bx, W, sq, bsq, rs, brs, tmp_ring, [(lambda k: hf[:, k, 16:16 + W], bhf)])
                for k in range(KC):
                    gi = k // 2
                    cur = hf[:, k, :]
                    bcur = bhf
                    for j in range(gi + 1):
                        sp_ = 1 << j
                        lo = 2 * sp_ - 1
                        ring = sa_ring if j % 2 == 0 else sb_ring
                        nx, bnx = ring.next()
                        eng = "pool" if (j % 2 == 0) else "dve"
                        S.op(eng, lambda e, nx=nx, cur=cur, lo=lo, sp_=sp_: e.tensor_tensor(
                            out=nx[:, lo:W + 16], in0=cur[:, lo:W + 16], in1=cur[:, lo - sp_:W + 16 - sp_], op=ALU.add),
                            reads=[bcur], writes=[bnx])
                        cur, bcur = nx, bnx
                    w = 2 << gi
                    S.op("dve", lambda e, cur=cur, k=k, w=w: e.scalar_tensor_tensor(
                        out=pl[:, k, :], in0=cur[:, 16:16 + W], scalar=1.0 / w, in1=hf[:, k, 16:16 + W],
                        op0=ALU.mult, op1=ALU.subtract), reads=[bcur, bhf], writes=[bpl])
                    if t0 == 0:
                        tt, tb = tmp_ring.next()
                        S.op("dve", lambda e, cur=cur, k=k, tt=tt: e.tensor_tensor(out=tt[:, 0:16], in0=cur[:, 16:32], in1=icnt[:, k, :], op=ALU.mult),
                             reads=[bcur, bcn], writes=[tb])
                        S.op("dve", lambda e, k=k, tt=tt: e.tensor_tensor(out=pl[:, k, 0:16], in0=tt[:, 0:16], in1=hf[:, k, 16:32], op=ALU.subtract),
                             reads=[tb, bhf], writes=[bpl])
                for n in range(KC):
                    gi, dd = n // 2, n % 2
                    py, bpy = py_ring.next()
                    for cc in range(2):
                        S.op("pe", lambda e, py=py, gi=gi, dd=dd, cc=cc: e.matmul(
                            py[:, 0:W], lhsT=pw[:, gi, cc, dd * 128:(dd + 1) * 128], rhs=pl[:, 2 * gi + cc, :],
                            start=(cc == 0), stop=(cc == 1)), reads=[bpw, bpl], writes=[bpy])
                    S.op("dve", lambda e, py=py, n=n: e.scalar_tensor_tensor(
                        out=xb[:, n, :], in0=py[:, 0:W], scalar=gps[:, n:n + 1], in1=xb[:, n, :],
                        op0=ALU.mult, op1=ALU.add), reads=[bpy, bcn, bx], writes=[bx])
                for k in range(KC):
                    S.op("pool", lambda e, k=k: e.tensor_copy(out=hf[:, k, 0:16], in_=hf[:, k, W:W + 16]), reads=[bhf], writes=[bhf])
                S.op("sp", lambda e, t0=t0: e.dma_start(out=xview(dst, t0, W), in_=xb),
                     reads=[bx], writes=xs_bufs(t0, W), dma=True)
            cur_src[0] = dst

        def s5_phase(i, dst):
            S.barrier()
            AR.reset()
            T = 256
            src = x_src()
            TWO_PI = 6.283185
            sm = lambda: AR.alloc([128, 32], F32)
            lre, lim, ldt, dtt, ard, aid, mag, sn, cs_, kre, kim = [sm() for _ in range(11)]
            cT_, sT_, Xtr, Xti, INr, INi, t32a, t32b, t32c, t32d = [sm() for _ in range(10)]
            b_p = Buf(S, "s5pre")
            BreT = AR.alloc([128, 32, 128], BF16)
            BimT = AR.alloc([128, 32, 128], BF16)
            CreT = AR.alloc([128, 32, 128], BF16)
            CimTn = AR.alloc([128, 32, 128], BF16)
            b_W = Buf(S, "s5W")
            Ec = AR.alloc([128, 32, T], F32)
            Es = AR.alloc([128, 32, T], F32)
            b_E = Buf(S, "s5E")
            wglu = AR.alloc([128, KC, D], BF16)
            b_wg = Buf(S)
            dsk = AR.alloc([128, KC], F32)
            iot = AR.alloc([128, T], F32)
            mark = AR.off
            bre = AR.alloc([128, 32, 16], F32)
            bim = AR.alloc([128, 32, 16], F32)
            cre = AR.alloc([128, 32, 16], F32)
            cim = AR.alloc([128, 32, 16], F32)
            bbr = AR.alloc([128, 32, 16], F32)
            bbi = AR.alloc([128, 32, 16], F32)
            tb1 = AR.alloc([128, 32, 16], F32)
            i32t = AR.alloc([128, T], I32)
            fA = AR.alloc([128, T], F32)
            fB = AR.alloc([128, T], F32)
            fC = AR.alloc([128, T], F32)
            ms_ring = Ring(S, [AR.alloc([128, 128], F32) for _ in range(4)])
            for (dst_ap, src_ap) in [(lre, s5_lre), (lim, s5_lim), (ldt, s5_ldt), (bre, s5_bre), (bim, s5_bim), (cre, s5_cre),
                                     (cim, s5_cim), (dsk, s5_dT), (iot, iota_in[:, 0:T])]:
                S.op("sp", lambda e, a=dst_ap, b=src_ap: e.dma_start(out=a, in_=b), writes=[b_p], dma=True)
            S.op("sp", lambda e: e.dma_start(out=wglu, in_=glu_bf.rearrange("(k p) n -> p k n", p=128)), reads=[b_mixw["glu"]], writes=[b_wg], dma=True)

            def P(eng, fn, extra_r=(), extra_w=()):
                S.op(eng, fn, reads=[b_p] + list(extra_r), writes=[b_p] + list(extra_w))

            def sincos(theta, n, out_s, out_c):
                a, b_, c_, ii = fA[:, 0:n], fB[:, 0:n], fC[:, 0:n], i32t[:, 0:n]
                P("dve", lambda e: e.tensor_scalar(out=a, in0=theta, scalar1=1.0 / (2 * math.pi), scalar2=None, op0=ALU.mult))
                P("dve", lambda e: e.tensor_copy(out=ii, in_=a))
                P("dve", lambda e: e.tensor_copy(out=b_, in_=ii))
                P("dve", lambda e: e.tensor_tensor(out=a, in0=a, in1=b_, op=ALU.subtract))
                P("dve", lambda e: e.tensor_scalar(out=b_, in0=a, scalar1=0.5, scalar2=None, op0=ALU.is_gt))
                P("dve", lambda e: e.tensor_tensor(out=a, in0=a, in1=b_, op=ALU.subtract))
                P("dve", lambda e: e.tensor_scalar(out=b_, in0=a, scalar1=-0.5, scalar2=None, op0=ALU.is_lt))
                P("dve", lambda e: e.tensor_tensor(out=a, in0=a, in1=b_, op=ALU.add))
                P("act", lambda e: e.activation(out=out_s, in_=a, func=AF.Sin, scale=TWO_PI))
                P("dve", lambda e: e.tensor_scalar(out=c_, in0=a, scalar1=0.25, scalar2=None, op0=ALU.add))
                P("dve", lambda e: e.tensor_scalar(out=b_, in0=c_, scalar1=0.5, scalar2=None, op0=ALU.is_gt))
                P("dve", lambda e: e.tensor_tensor(out=c_, in0=c_, in1=b_, op=ALU.subtract))
                P("act", lambda e: e.activation(out=out_c, in_=c_, func=AF.Sin, scale=TWO_PI))

            tt = lambda o, a, b, op: P("dve", lambda e: e.tensor_tensor(out=o, in0=a, in1=b, op=op))
            P("act", lambda e: e.activation(out=dtt, in_=ldt, func=AF.Exp))
            tt(ard, lre, dtt, ALU.mult)
            tt(aid, lim, dtt, ALU.mult)
            P("act", lambda e: e.activation(out=mag, in_=ard, func=AF.Exp))
            sincos(aid, 32, sn, cs_)
            P("dve", lambda e: e.tensor_scalar(out=t32a, in0=aid, scalar1=float(T), scalar2=None, op0=ALU.mult))
            sincos(t32a, 32, sT_, cT_)
            tt(t32a, mag, cs_, ALU.mult)
            tt(t32b, mag, sn, ALU.mult)
            P("dve", lambda e: e.tensor_scalar(out=t32a, in0=t32a, scalar1=-1.0, scalar2=None, op0=ALU.add))
            tt(t32c, lre, lre, ALU.mult)
            tt(t32d, lim, lim, ALU.mult)
            tt(t32c, t32c, t32d, ALU.add)
            P("dve", lambda e: e.reciprocal(out=t32c, in_=t32c))
            tt(kre, t32a, lre, ALU.mult)
            tt(t32d, t32b, lim, ALU.mult)
            tt(kre, kre, t32d, ALU.add)
            tt(kre, kre, t32c, ALU.mult)
            tt(kim, t32b, lre, ALU.mult)
            tt(t32d, t32a, lim, ALU.mult)
            tt(kim, kim, t32d, ALU.subtract)
            tt(kim, kim, t32c, ALU.mult)
            bc = lambda a: a.unsqueeze(2).to_broadcast([128, 32, 16])
            tt(bbr, bre, bc(kre), ALU.mult)
            tt(tb1, bim, bc(kim), ALU.mult)
            tt(bbr, bbr, tb1, ALU.subtract)
            tt(bbi, bim, bc(kre), ALU.mult)
            tt(tb1, bre, bc(kim), ALU.mult)
            tt(bbi, bbi, tb1, ALU.add)
            P("dve", lambda e: e.memset(CreT, 0.0), extra_w=[b_W])
            P("dve", lambda e: e.memset(CimTn, 0.0), extra_w=[b_W])
            P("dve", lambda e: e.memset(Xtr, 0.0))
            P("dve", lambda e: e.memset(Xti, 0.0))
            P("dve", lambda e: e.memset(INr, 0.0))
            P("dve", lambda e: e.memset(INi, 0.0))
            ptr_ring = Ring(S, [])
            ptr_ring.items = [(psum[0], PB[0]), (psum[1], PB[1])]
            def tile_pre(ti):
                for (bb, dstT) in [(bbr, BreT), (bbi, BimT)]:
                    ms, bms = ms_ring.next()
                    S.op("pool", lambda e, ms=ms: e.memset(ms, 0.0), writes=[bms])
                    for gl in range(2):
                        c0 = ((ti % 4) * 2 + gl) * 16
                        S.op("pool", lambda e, ms=ms, gl=gl, c0=c0, bb=bb: e.tensor_copy(
                            out=ms[gl * 64:(gl + 1) * 64, c0:c0 + 16], in_=bb[gl * 64:(gl + 1) * 64, ti, :]), reads=[b_p], writes=[bms])
                    pt, bpt = ptr_ring.next()
                    S.op("pe", lambda e, pt=pt, ms=ms: e.transpose(out=pt[:, 0:128], in_=ms, identity=ident), reads=[bms, b_c], writes=[bpt])
                    S.op("act", lambda e, pt=pt, dstT=dstT: e.activation(out=dstT[:, ti, :], in_=pt[:, 0:128], func=AF.Identity), reads=[bpt], writes=[b_W])
                for gl in range(2):
                    c0 = ((ti % 4) * 2 + gl) * 16
                    S.op("dve", lambda e, gl=gl, c0=c0: e.tensor_copy(out=CreT[gl * 64:(gl + 1) * 64, ti, c0:c0 + 16],
                                                                      in_=cre[gl * 64:(gl + 1) * 64, ti, :]), reads=[b_p], writes=[b_W])
                    S.op("dve", lambda e, gl=gl, c0=c0: e.tensor_scalar(out=CimTn[gl * 64:(gl + 1) * 64, ti, c0:c0 + 16],
                                                                        in0=cim[gl * 64:(gl + 1) * 64, ti, :], scalar1=-1.0, scalar2=None, op0=ALU.mult),
                         reads=[b_p], writes=[b_W])
                P("dve", lambda e: e.tensor_scalar(out=fC[:, 0:T], in0=iot, scalar1=aid[:, ti:ti + 1], scalar2=None, op0=ALU.mult))
                P("dve", lambda e: e.tensor_copy(out=Ec[:, ti, :], in_=fC[:, 0:T]), extra_w=[b_E])
                sincos(Ec[:, ti, :], T, Es[:, ti, :], Ec[:, ti, :])
            for ti_ in range(32):
                tile_pre(ti_)
            if DEBUG:
                b_dbg = Buf(S)
                items = [(mag, 32), (sn, 32), (cs_, 32), (kre, 32), (kim, 32), (cT_, 32), (sT_, 32), (bbr.rearrange("p a b -> p (a b)"), 512),
                         (bbi.rearrange("p a b -> p (a b)"), 512), (Ec[:, 5, :], T), (Es[:, 5, :], T), (aid, 32)]
                o = 0
                for (a, n) in items:
                    S.op("sp", lambda e, a=a, o=o, n=n: e.dma_start(out=dbg[:, o:o + n], in_=a), reads=[b_p, b_E], writes=[b_dbg], dma=True)
                    o += n
            S.barrier()
            AR.off = mark
            xb = AR.alloc([128, KC, T], F32)
            bx = Buf(S)
            hf = AR.alloc([128, KC, T], F32)
            bhf = Buf(S)
            hb = AR.alloc([128, KC, T], BF16)
            bhb = Buf(S)
            gf = AR.alloc([128, KC, T], F32)
            bgf = Buf(S)
            gb = AR.alloc([128, KC, T], BF16)
            bgb = Buf(S)
            sq = AR.alloc([128, KC, T], BF16)
            bsq = Buf(S)
            rs = AR.alloc([128, T], F32)
            brs = Buf(S)
            tmp_ring = Ring(S, [AR.alloc([128, T], F32) for _ in range(2)])
            vr_ring = Ring(S, [AR.alloc([128, T], F32) for _ in range(2)])
            vi_ring = Ring(S, [AR.alloc([128, T], F32) for _ in range(2)])
            m_ring = Ring(S, [AR.alloc([128, T], F32) for _ in range(4)])
            w_ring = Ring(S, [AR.alloc([128, T], F32) for _ in range(4)])
            xt_ring = Ring(S, [AR.alloc([128, T], F32) for _ in range(4)])
            p_ring = Ring(S, [AR.alloc([128, T], BF16) for _ in range(8)])
            yf_ring = Ring(S, [AR.alloc([128, T], F32) for _ in range(2)])
            sg_ring = Ring(S, [AR.alloc([128, T], F32) for _ in range(2)])
            b_X = Buf(S, "Xt")
            b_IN = Buf(S, "INIT")
            pvr_ring = Ring(S, [])
            pvr_ring.items = [(psum[0], PB[0]), (psum[1], PB[1])]
            pvi_ring = Ring(S, [])
            pvi_ring.items = [(psum[2], PB[2]), (psum[3], PB[3])]
            py_ring = Ring(S, [])
            py_ring.items = [(psum[4], PB[4]), (psum[5], PB[5])]
            for t0 in range(0, SEQ, T):
                S.op("sp", lambda e, t0=t0: e.dma_start(out=xb, in_=xview(src, t0, T)), reads=xs_bufs(t0, T), writes=[bx], dma=True)
                prenorm(i, 1, xb, bx, T, sq, bsq, rs, brs, tmp_ring, [(lambda k: hf[:, k, :], bhf), (lambda k: hb[:, k, :], bhb)])
                for c in range(KC):
                    py, bpy = py_ring.next()
                    for q in range(4):
                        ti = c * 4 + q
                        pvr, bpvr = pvr_ring.next()
                        pvi, bpvi = pvi_ring.next()
                        S.op("pe", lambda e, pvr=pvr, ti=ti, c=c: e.matmul(pvr[:, 0:T], lhsT=BreT[:, ti, :], rhs=hb[:, c, :], start=True, stop=True),
                             reads=[b_W, bhb], writes=[bpvr])
                        S.op("pe", lambda e, pvi=pvi, ti=ti, c=c: e.matmul(pvi[:, 0:T], lhsT=BimT[:, ti, :], rhs=hb[:, c, :], start=True, stop=True),
                             reads=[b_W, bhb], writes=[bpvi])
                        vr, bvr = vr_ring.next()
                        vi, bvi = vi_ring.next()
                        S.op("act", lambda e, vr=vr, pvr=pvr: e.activation(out=vr, in_=pvr[:, 0:T], func=AF.Identity), reads=[bpvr], writes=[bvr])
                        S.op("act", lambda e, vi=vi, pvi=pvi: e.activation(out=vi, in_=pvi[:, 0:T], func=AF.Identity), reads=[bpvi], writes=[bvi])
                        m1, bm1 = m_ring.next()
                        m2, bm2 = m_ring.next()
                        wr, bwr = w_ring.next()
                        S.op("pool", lambda e, m1=m1, vr=vr, ti=ti: e.tensor_tensor(out=m1, in0=vr, in1=Ec[:, ti, :], op=ALU.mult), reads=[bvr, b_E], writes=[bm1])
                        S.op("pool", lambda e, m2=m2, vi=vi, ti=ti: e.tensor_tensor(out=m2, in0=vi, in1=Es[:, ti, :], op=ALU.mult), reads=[bvi, b_E], writes=[bm2])
                        S.op("pool", lambda e, wr=wr, m1=m1, m2=m2: e.tensor_tensor(out=wr, in0=m1, in1=m2, op=ALU.add), reads=[bm1, bm2], writes=[bwr])
                        m3, bm3 = m_ring.next()
                        m4, bm4 = m_ring.next()
                        wi, bwi = w_ring.next()
                        S.op("pool", lambda e, m3=m3, vi=vi, ti=ti: e.tensor_tensor(out=m3, in0=vi, in1=Ec[:, ti, :], op=ALU.mult), reads=[bvi, b_E], writes=[bm3])
                        S.op("pool", lambda e, m4=m4, vr=vr, ti=ti: e.tensor_tensor(out=m4, in0=vr, in1=Es[:, ti, :], op=ALU.mult), reads=[bvr, b_E], writes=[bm4])
                        S.op("pool", lambda e, wi=wi, m3=m3, m4=m4: e.tensor_tensor(out=wi, in0=m3, in1=m4, op=ALU.subtract), reads=[bm3, bm4], writes=[bwi])
                        xtr, bxtr = xt_ring.next()
                        xti, bxti = xt_ring.next()
                        S.op("dve", lambda e, xtr=xtr, wr=wr, ti=ti: e.tensor_tensor_scan(
                            out=xtr, data0=mag[:, ti:ti + 1].to_broadcast([128, T]), data1=wr, initial=INr[:, ti:ti + 1], op0=ALU.mult, op1=ALU.add),
                            reads=[bwr, b_IN, b_p], writes=[bxtr])
                        S.op("dve", lambda e, xti=xti, wi=wi, ti=ti: e.tensor_tensor_scan(
                            out=xti, data0=mag[:, ti:ti + 1].to_broadcast([128, T]), data1=wi, initial=INi[:, ti:ti + 1], op0=ALU.mult, op1=ALU.add),
                            reads=[bwi, b_IN, b_p], writes=[bxti])
                        S.op("act", lambda e, xtr=xtr, ti=ti: e.activation(out=Xtr[:, ti:ti + 1], in_=xtr[:, T - 1:T], func=AF.Identity), reads=[bxtr], writes=[b_X])
                        S.op("act", lambda e, xti=xti, ti=ti: e.activation(out=Xti[:, ti:ti + 1], in_=xti[:, T - 1:T], func=AF.Identity), reads=[bxti], writes=[b_X])
                        p1, bp1 = p_ring.next()
                        p2, bp2 = p_ring.next()
                        p3, bp3 = p_ring.next()
                        p4, bp4 = p_ring.next()
                        S.op("dve", lambda e, p1=p1, xtr=xtr, ti=ti: e.tensor_tensor(out=p1, in0=xtr, in1=Ec[:, ti, :], op=ALU.mult), reads=[bxtr, b_E], writes=[bp1])
                        S.op("dve", lambda e, p2=p2, xti=xti, ti=ti: e.scalar_tensor_tensor(out=p2, in0=xti, scalar=-1.0, in1=Es[:, ti, :], op0=ALU.mult, op1=ALU.mult),
                             reads=[bxti, b_E], writes=[bp2])
                        S.op("dve", lambda e, p3=p3, xtr=xtr, ti=ti: e.tensor_tensor(out=p3, in0=xtr, in1=Es[:, ti, :], op=ALU.mult), reads=[bxtr, b_E], writes=[bp3])
                        S.op("dve", lambda e, p4=p4, xti=xti, ti=ti: e.tensor_tensor(out=p4, in0=xti, in1=Ec[:, ti, :], op=ALU.mult), reads=[bxti, b_E], writes=[bp4])
                        for pi_, (pp, bpp, WT) in enumerate([(p1, bp1, CreT), (p2, bp2, CreT), (p3, bp3, CimTn), (p4, bp4, CimTn)]):
                            S.op("pe", lambda e, py=py, pp=pp, WT=WT, ti=ti, pi_=pi_, q=q: e.matmul(
                                py[:, 0:T], lhsT=WT[:, ti, :], rhs=pp, start=(q == 0 and pi_ == 0), stop=(q == 3 and pi_ == 3)),
                                reads=[b_W, bpp], writes=[bpy])
                    yf, byf = yf_ring.next()
                    S.op("dve", lambda e, yf=yf, py=py, c=c: e.scalar_tensor_tensor(out=yf, in0=hf[:, c, :], scalar=dsk[:, c:c + 1], in1=py[:, 0:T],
                                                                                  op0=ALU.mult, op1=ALU.add), reads=[bhf, bpy, b_p], writes=[byf])
                    S.op("act", lambda e, yf=yf, c=c: e.activation(out=gf[:, c, :], in_=yf, func=AF.Gelu_apprx_tanh), reads=[byf], writes=[bgf])
                    S.op("pool", lambda e, c=c: e.tensor_copy(out=gb[:, c, :], in_=gf[:, c, :]), reads=[bgf], writes=[bgb])
                S.op("dve", lambda e: e.tensor_tensor(out=t32a, in0=cT_, in1=Xtr, op=ALU.mult), reads=[b_X, b_p], writes=[b_p])
                S.op("dve", lambda e: e.tensor_tensor(out=t32b, in0=sT_, in1=Xti, op=ALU.mult), reads=[b_X, b_p], writes=[b_p])
                S.op("dve", lambda e: e.tensor_tensor(out=INr, in0=t32a, in1=t32b, op=ALU.subtract), reads=[b_p], writes=[b_IN])
                S.op("dve", lambda e: e.tensor_tensor(out=t32a, in0=sT_, in1=Xtr, op=ALU.mult), reads=[b_X, b_p], writes=[b_p])
                S.op("dve", lambda e: e.tensor_tensor(out=t32b, in0=cT_, in1=Xti, op=ALU.mult), reads=[b_X, b_p], writes=[b_p])
                S.op("dve", lambda e: e.tensor_tensor(out=INi, in0=t32a, in1=t32b, op=ALU.add), reads=[b_p], writes=[b_IN])
                for n in range(KC):
                    pz, bpz = py_ring.next()
                    for k in range(KC):
                        S.op("pe", lambda e, pz=pz, n=n, k=k: e.matmul(pz[:, 0:T], lhsT=wglu[:, k, n * 128:(n + 1) * 128], rhs=gb[:, k, :],
                                                                       start=(k == 0), stop=(k == KC - 1)), reads=[b_wg, bgb], writes=[bpz])
                    sg, bsg = sg_ring.next()
                    S.op("act", lambda e, sg=sg, pz=pz: e.activation(out=sg, in_=pz[:, 0:T], func=AF.Sigmoid), reads=[bpz], writes=[bsg])
                    S.op("dve", lambda e, sg=sg, n=n: e.tensor_tensor(out=sg, in0=sg, in1=gf[:, n, :], op=ALU.mult), reads=[bsg, bgf], writes=[bsg])
                    S.op("dve", lambda e, sg=sg, n=n: e.scalar_tensor_tensor(out=xb[:, n, :], in0=sg, scalar=gp_col(i, 1, n), in1=xb[:, n, :],
                                                                             op0=ALU.mult, op1=ALU.add), reads=[bsg, b_mod, bx], writes=[bx])
                S.op("sp", lambda e, t0=t0: e.dma_start(out=xview(dst, t0, T), in_=xb), reads=[bx], writes=xs_bufs(t0, T), dma=True)
            cur_src[0] = dst

        def fox_phase(i, dst):
            S.barrier()
            AR.reset()
            W = 512
            src = x_src()
            FT_all = AR.alloc([128, NKB, 16], F32)
            b_FT = Buf(S, "FT")
            CB = AR.alloc([128, 16 * NQB], F32)
            b_CB = Buf(S, "CB")
            sel = AR.alloc([128, 64], F32)
            qgs = AR.alloc([128, 1], F32)
            kgs = AR.alloc([128, 1], F32)
            negb = AR.alloc([16, 1], F32)
            negone = AR.alloc([128, 1], F32)
            Fprev = AR.alloc([16, 1], F32)
            Ccol = AR.alloc([16, NQB], F32)
            ones16 = AR.alloc([16, W], F32)
            ones16b = AR.alloc([16, 2, W], BF16)
            b_k = Buf(S, "foxconst")
            b_Fp = Buf(S, "Fprev")
            b_Cc = Buf(S, "Ccol")
            S.op("sp", lambda e: e.dma_start(out=qgs, in_=fox_qg), writes=[b_k], dma=True)
            S.op("sp", lambda e: e.dma_start(out=kgs, in_=fox_kg), writes=[b_k], dma=True)
            S.op("sp", lambda e: e.dma_start(out=negb, in_=fox_bf), writes=[b_k], dma=True)
            S.op("dve", lambda e: e.tensor_scalar(out=qgs, in0=qgs, scalar1=0.125, scalar2=None, op0=ALU.mult), reads=[b_k], writes=[b_k])
            S.op("dve", lambda e: e.tensor_scalar(out=negb, in0=negb, scalar1=-1.0, scalar2=None, op0=ALU.mult), reads=[b_k], writes=[b_k])
            S.op("dve", lambda e: e.memset(sel[0:64, :], 0.0), writes=[b_k])
            S.op("dve", lambda e: e.memset(sel[64:128, :], 0.0), writes=[b_k])
            S.op("dve", lambda e: e.memset(sel[64:65, :], 1.0), reads=[b_k], writes=[b_k])
            S.op("dve", lambda e: e.memset(Fprev, 0.0), writes=[b_Fp])
            S.op("dve", lambda e: e.memset(negone, -1.0), writes=[b_k])
            S.op("dve", lambda e: e.memset(ones16, 1.0), writes=[b_k])
            S.op("dve", lambda e: e.memset(ones16b, 1.0), writes=[b_k])
            mark = AR.off
            win = AR.alloc([128, KC, 3088], BF16)
            b_win_sb = Buf(S)
            for k in range(KC):
                S.op("sp", lambda e, k=k: e.dma_start(out=win[:, k, :], in_=foxin_bf[k * 128:(k + 1) * 128, :]),
                     reads=[b_mixw["foxin"]], writes=[b_win_sb], dma=True)
            xb = AR.alloc([128, KC, W], F32)
            bx = Buf(S)
            h = AR.alloc([128, KC, W], BF16)
            bh = Buf(S)
            sq = AR.alloc([128, KC, W], BF16)
            bsq = Buf(S)
            rs = AR.alloc([128, W], F32)
            brs = Buf(S)
            tmp_ring = Ring(S, [AR.alloc([128, W], F32) for _ in range(2)])
            qs_ring = Ring(S, [AR.alloc([128, W], BF16) for _ in range(2)])
            rq_ring = Ring(S, [AR.alloc([128, W], F32) for _ in range(2)])
            qn_ring = Ring(S, [AR.alloc([128, W], BF16) for _ in range(3)])
            vsb_ring = Ring(S, [AR.alloc([128, 1024], BF16) for _ in range(2)])
            lf = AR.alloc([16, W], F32)
            Fblk = AR.alloc([16, W], F32)
            Frel = AR.alloc([16, W], F32)
            hif = AR.alloc([16, W], F32)
            hib = AR.alloc([16, W], BF16)
            lob = AR.alloc([16, W], BF16)
            b_f = Buf(S, "fwork")
            pq_ring = Ring(S, [])
            pq_ring.items = [(psum[0], PB[0]), (psum[1], PB[1])]
            pss_ring = Ring(S, [])
            pss_ring.items = [(psum[2], PB[2]), (psum[3], PB[3])]
            pv_ring = Ring(S, [])
            pv_ring.items = [(psum[4], PB[4]), (psum[5], PB[5])]
            pf, bpf = psum[7], PB[7]
            b_qT = Buf(S, "qTa")
            b_kT = Buf(S, "kTa")
            b_vS = Buf(S, "vS")
            for bi, t0 in enumerate(range(0, SEQ, W)):
                S.op("sp", lambda e, t0=t0: e.dma_start(out=xb, in_=xview(src, t0, W)), reads=xs_bufs(t0, W), writes=[bx], dma=True)
                prenorm(i, 1, xb, bx, W, sq, bsq, rs, brs, tmp_ring, [(lambda k: h[:, k, :], bh)])
                for (coff, gsc, dT, bdT) in ([] if 'a' in FOX_DBG else [(0, qgs, qTa, b_qT), (D, kgs, kTa, b_kT)]):
                    for m in range(KC):
                        pq, bpq = pq_ring.next()
                        for k in range(KC):
                            S.op("pe", lambda e, pq=pq, k=k, m=m, coff=coff: e.matmul(
                                pq[:, :], lhsT=win[:, k, coff + m * 128:coff + (m + 1) * 128], rhs=h[:, k, :], start=(k == 0), stop=(k == KC - 1)),
                                reads=[b_win_sb, bh], writes=[bpq])
                        qs, bqs = qs_ring.next()
                        S.op("act", lambda e, qs=qs, pq=pq: e.activation(out=qs, in_=pq[:, :], func=AF.Square), reads=[bpq], writes=[bqs])
                        pss, bpss = pss_ring.next()
                        S.op("pe", lambda e, pss=pss, qs=qs: e.matmul(pss[:, :], lhsT=bones_bf, rhs=qs, start=True, stop=True), reads=[bqs, b_c], writes=[bpss])
                        rq, brq = rq_ring.next()
                        S.op("act", lambda e, rq=rq, pss=pss: e.activation(out=rq, in_=pss[:, :], func=AF.Sqrt, bias=EPS, scale=1.0 / 64), reads=[bpss], writes=[brq])
                        S.op("dve", lambda e, rq=rq: e.reciprocal(out=rq, in_=rq), reads=[brq], writes=[brq])
                        qn, bqn = qn_ring.next()
                        S.op("dve", lambda e, qn=qn, pq=pq, rq=rq, gsc=gsc: e.scalar_tensor_tensor(out=qn, in0=pq[:, :], scalar=gsc[:, 0:1], in1=rq,
                                                                                               op0=ALU.mult, op1=ALU.mult), reads=[bpq, brq, b_k], writes=[bqn])
                        for hh in range(2):
                            S.op("sp", lambda e, qn=qn, hh=hh, m=m, t0=t0, dT=dT: e.dma_start(out=dT[2 * m + hh, 0:64, t0:t0 + W], in_=qn[hh * 64:(hh + 1) * 64, :]),
                                 reads=[bqn], writes=[bdT], dma=True)
                for tt in range(0 if 'b' in FOX_DBG else 4):
                    vsb, bvsb = vsb_ring.next()
                    for hf_ in range(2):
                        pv, bpv = pv_ring.next()
                        for k in range(KC):
                            S.op("pe", lambda e, pv=pv, k=k, tt=tt, hf_=hf_: e.matmul(
                                pv[:, :], lhsT=h[:, k, tt * 128:(tt + 1) * 128], rhs=win[:, k, 2 * D + hf_ * 512:2 * D + (hf_ + 1) * 512],
                                start=(k == 0), stop=(k == KC - 1)), reads=[b_win_sb, bh], writes=[bpv])
                        if hf_ == 0:
                            S.op("act", lambda e, pv=pv, vsb=vsb: e.activation(out=vsb[:, 0:512], in_=pv[:, :], func=AF.Identity), reads=[bpv], writes=[bvsb])
                        else:
                            S.op("dve", lambda e, pv=pv, vsb=vsb: e.tensor_copy(out=vsb[:, 512:1024], in_=pv[:, :]), reads=[bpv], writes=[bvsb])
                    kb = t0 // 128 + tt
                    S.op("sp", lambda e, vsb=vsb, kb=kb: e.dma_start(out=vS[:, kb].rearrange("h p d -> p h d"), in_=vsb.rearrange("p (h d) -> p h d", h=16)),
                         reads=[bvsb], writes=[b_vS], dma=True)
                if 'c' in FOX_DBG:
                    continue
                for k in range(KC):
                    S.op("pe", lambda e, k=k: e.matmul(pf[0:16, :], lhsT=win[:, k, 3 * D:3 * D + 16], rhs=h[:, k, :], start=(k == 0), stop=(k == KC - 1)),
                         reads=[b_win_sb, bh], writes=[bpf])
                S.op("act", lambda e: e.activation(out=lf, in_=pf[0:16, :], func=AF.Exp, bias=negb[:, 0:1], scale=-1.0), reads=[bpf, b_k], writes=[b_f])
                S.op("act", lambda e: e.activation(out=lf, in_=lf, func=AF.Ln, bias=1.0, scale=1.0), reads=[b_f], writes=[b_f])
                S.op("dve", lambda e: e.tensor_tensor_scan(out=Fblk, data0=ones16, data1=lf, initial=Fprev[:, 0:1], op0=ALU.mult, op1=ALU.subtract),
                     reads=[b_f, b_Fp, b_k], writes=[b_f])
                S.op("dve", lambda e: e.tensor_copy(out=Fprev, in_=Fblk[:, W - 1:W]), reads=[b_f], writes=[b_Fp])
                S.op("dve", lambda e, bi=bi: e.tensor_copy(out=Ccol[:, bi:bi + 1], in_=Fblk[:, 0:1]), reads=[b_f], writes=[b_Cc])
                S.op("dve", lambda e: e.tensor_scalar(out=Frel, in0=Fblk, scalar1=Fblk[:, 0:1], scalar2=None, op0=ALU.subtract), reads=[b_f], writes=[b_f])
                S.op("dve", lambda e: e.tensor_copy(out=hib, in_=Frel), reads=[b_f], writes=[b_f])
                S.op("dve", lambda e: e.tensor_copy(out=hif, in_=hib), reads=[b_f], writes=[b_f])
                S.op("dve", lambda e: e.tensor_tensor(out=lob, in0=Frel, in1=hif, op=ALU.subtract), reads=[b_f], writes=[b_f])
                S.op("sp", lambda e, t0=t0: e.dma_start(out=qTa[:, 64, t0:t0 + W], in_=hib), reads=[b_f], writes=[b_qT], dma=True)
                S.op("sp", lambda e, t0=t0: e.dma_start(out=qTa[:, 65, t0:t0 + W], in_=lob), reads=[b_f], writes=[b_qT], dma=True)
                S.op("sp", lambda e, t0=t0: e.dma_start(out=kTa[:, 64:66, t0:t0 + W], in_=ones16b), reads=[b_k], writes=[b_kT], dma=True)
                for tt in range(4):
                    pt_, bpt_ = pv_ring.next()
                    kb = t0 // 128 + tt
                    S.op("pe", lambda e, pt_=pt_, tt=tt: e.transpose(out=pt_[:, 0:16], in_=Fblk[0:16, tt * 128:(tt + 1) * 128], identity=ident[0:16, 0:16]),
                         reads=[b_f, b_c], writes=[bpt_])
                    S.op("act", lambda e, pt_=pt_, kb=kb: e.activation(out=FT_all[:, kb, :], in_=pt_[:, 0:16], func=AF.Identity), reads=[bpt_], writes=[b_FT])
            if DEBUG == 3:
                b_dbg = Buf(S)
                cvt = AR.alloc([128, 512], F32)
                for n_, (srcap, bsrc) in enumerate([(h[:, 0, :], bh), (win[:, 0, 0:512], b_win_sb), (win[:, 3, 2048:2560], b_win_sb), (sq[:, 0, :], bsq)]):
                    S.op("dve", lambda e, srcap=srcap: e.tensor_copy(out=cvt, in_=srcap), reads=[bsrc, b_dbg], writes=[b_dbg])
                    o_ = n_ * 512
                    S.op("sp", lambda e, o_=o_: e.dma_start(out=dbg[:, o_:o_ + 512], in_=cvt), reads=[b_dbg], writes=[b_dbg], dma=True)
                S.op("sp", lambda e: e.dma_start(out=dbg[:, 2048:2560], in_=xb[:, 0, :]), reads=[bx], writes=[b_dbg], dma=True)
                S.op("sp", lambda e: e.dma_start(out=dbg[:, 2560:3072], in_=rs), reads=[brs], writes=[b_dbg], dma=True)
            b_cD = Buf(S, "cD")
            S.op("sp", lambda e: e.dma_start(out=cD, in_=Ccol), reads=[b_Cc], writes=[b_cD], dma=True)
            S.op("sp", lambda e: e.dma_start(out=CB, in_=cD.rearrange("h i -> (h i)").unsqueeze(0).to_broadcast([128, 16 * NQB])),
                 reads=[b_cD], writes=[b_CB], dma=True)
            S.barrier()
            AR.off = mark
            QT_ring = Ring(S, [AR.alloc([128, SEQ], BF16) for _ in range(2)])
            KT_ring = Ring(S, [AR.alloc([128, SEQ], BF16) for _ in range(2)])
            VA_ring = Ring(S, [AR.alloc([128, NKB, 128], BF16) for _ in range(2)])
            Bias_ring = Ring(S, [AR.alloc([128, NKB, NQB], F32) for _ in range(2)])
            pt_ring = Ring(S, [AR.alloc([128, W], BF16) for _ in range(4)])
            osb_ring = Ring(S, [AR.alloc([128, W], F32) for _ in range(2)])
            rd_ring = Ring(S, [AR.alloc([64, W], F32) for _ in range(2)])
            on_ring = Ring(S, [AR.alloc([64, W], BF16) for _ in range(2)])
            for (va, bva) in VA_ring.items:
                S.op("pool", lambda e, va=va: e.memset(va, 1.0), writes=[bva])
            for (qt, bqt) in QT_ring.items + KT_ring.items:
                S.op("pool", lambda e, qt=qt: e.memset(qt[64:128, :], 0.0), writes=[bqt])
            ps_ring = Ring(S, [])
            ps_ring.items = [(psum[q], PB[q]) for q in range(4)]
            po_ring = Ring(S, [])
            po_ring.items = [(psum[4], PB[4]), (psum[5], PB[5])]
            b_oT = Buf(S, "oTs")
            for hd in range({1: 0, 2: 1, 3: 16, 5: 16}[FOX_STAGE]):
                QT, bQT = QT_ring.next()
                KT, bKT = KT_ring.next()
                VA, bVA = VA_ring.next()
                Bs, bBs = Bias_ring.next()
                S.op("sp", lambda e, QT=QT, hd=hd: e.dma_start(out=QT[0:66, :], in_=qTa[hd]), reads=[b_qT], writes=[bQT], dma=True)
                S.op("sp", lambda e, KT=KT, hd=hd: e.dma_start(out=KT[0:66, :], in_=kTa[hd]), reads=[b_kT], writes=[bKT], dma=True)
                for k0 in range(0, NKB, 8):
                    S.op("sp", lambda e, VA=VA, hd=hd, k0=k0: e.dma_start(out=VA[:, k0:k0 + 8, 0:64], in_=vS[hd, k0:k0 + 8].rearrange("kb p d -> p kb d")),
                         reads=[b_vS], writes=[bVA], dma=True)
                for ib in range(NQB):
                    S.op("dve", lambda e, Bs=Bs, ib=ib, hd=hd: e.tensor_scalar(out=Bs[:, :, ib], in0=FT_all[:, :, hd], scalar1=negone[:, 0:1],
                                                                               scalar2=CB[:, hd * NQB + ib:hd * NQB + ib + 1], op0=ALU.mult, op1=ALU.add),
                         reads=[b_FT, b_CB, b_k], writes=[bBs])
                for ib in range(NQB if FOX_STAGE != 5 else 0):
                    nkb = 4 * (ib + 1)
                    po, bpo = po_ring.next()
                    for j in range(nkb):
                        jj = j - 4 * ib
                        qlo = 128 * jj if jj >= 0 else 0
                        N = W - qlo
                        ps, bps = ps_ring.next()
                        S.op("pe", lambda e, ps=ps, KT=KT, QT=QT, j=j, ib=ib, qlo=qlo, N=N: e.matmul(
                            ps[:, 0:N], lhsT=KT[:, j * 128:(j + 1) * 128], rhs=QT[:, ib * W + qlo:(ib + 1) * W], start=True, stop=True),
                            reads=[bKT, bQT], writes=[bps])
                        pt, bpt = pt_ring.next()
                        S.op("act", lambda e, pt=pt, ps=ps, Bs=Bs, j=j, ib=ib, N=N: e.activation(
                            out=pt[:, 0:N], in_=ps[:, 0:N], func=AF.Exp, bias=Bs[:, j, ib:ib + 1], scale=1.0), reads=[bps, bBs], writes=[bpt])
                        if jj >= 0:
                            S.op("pool", lambda e, pt=pt: e.tensor_tensor(out=pt[:, 0:128], in0=pt[:, 0:128], in1=tri_bf, op=ALU.mult),
                                 reads=[bpt, b_c], writes=[bpt])
                        S.op("pe", lambda e, po=po, VA=VA, pt=pt, j=j, qlo=qlo, N=N, nkb=nkb: e.matmul(
                            po[:, qlo:W], lhsT=VA[:, j, :], rhs=pt[:, 0:N], start=(j == 0), stop=(j == nkb - 1)),
                            reads=[bVA, bpt], writes=[bpo])
                    osb, bosb = osb_ring.next()
                    S.op("act", lambda e, osb=osb, po=po: e.activation(out=osb, in_=po[:, :], func=AF.Identity), reads=[bpo], writes=[bosb])
                    rd, brd = rd_ring.next()
                    S.op("dve", lambda e, rd=rd, osb=osb: e.tensor_copy(out=rd, in_=osb[64:128, :]), reads=[bosb], writes=[brd])
                    S.op("dve", lambda e, rd=rd: e.reciprocal(out=rd, in_=rd), reads=[brd], writes=[brd])
                    on, bon = on_ring.next()
                    S.op("dve", lambda e, on=on, osb=osb, rd=rd: e.tensor_tensor(out=on, in0=osb[0:64, :], in1=rd, op=ALU.mult), reads=[bosb, brd], writes=[bon])
                    S.op("sp", lambda e, on=on, hd=hd, ib=ib: e.dma_start(out=oTs[hd, :, ib * W:(ib + 1) * W], in_=on), reads=[bon], writes=[b_oT], dma=True)
            if DEBUG == 2:
                b_dbg = Buf(S)
                cvt = AR.alloc([128, 512], F32)
                S.op("sp", lambda e: e.dma_start(out=dbg[:, 0:32], in_=CB[:, 0:32]), reads=[b_CB], writes=[b_dbg], dma=True)
                S.op("sp", lambda e: e.dma_start(out=dbg[:, 32:160], in_=FT_all.rearrange("p a b -> p (a b)")[:, 0:128]), reads=[b_FT], writes=[b_dbg], dma=True)
                for n_, (srcap, bsrc) in enumerate([(QT_ring.items[1][0][:, 0:512], QT_ring.items[1][1]), (KT_ring.items[1][0][:, 0:512], KT_ring.items[1][1]),
                                                   (osb_ring.items[1][0], osb_ring.items[1][1]), (VA_ring.items[1][0].rearrange("p a b -> p (a b)")[:, 0:512], VA_ring.items[1][1])]):
                    S.op("dve", lambda e, srcap=srcap: e.tensor_copy(out=cvt, in_=srcap), reads=[bsrc, b_dbg], writes=[b_dbg])
                    o_ = 160 + n_ * 512
                    S.op("sp", lambda e, o_=o_: e.dma_start(out=dbg[:, o_:o_ + 512], in_=cvt), reads=[b_dbg], writes=[b_dbg], dma=True)
                S.op("sp", lambda e: e.dma_start(out=dbg[:, 2208:2272], in_=Bias_ring.items[1][0].rearrange("p a b -> p (a b)")[:, 0:16].to_broadcast([128, 16]) if False else Bias_ring.items[1][0].rearrange("p a b -> p (a b)")[:, 0:16]), reads=[Bias_ring.items[1][1]], writes=[b_dbg], dma=True) if False else None
            S.barrier()
            AR.off = mark
            xc = AR.alloc([128, KC, W], F32)
            bxc = Buf(S)
            oc = AR.alloc([128, KC, W], BF16)
            boc = Buf(S)
            wo = AR.alloc([128, KC, D], BF16)
            bwo_sb = Buf(S)
            S.op("sp", lambda e: e.dma_start(out=wo, in_=foxo_bf.rearrange("(k p) n -> p k n", p=128)), reads=[b_mixw["foxo"]], writes=[bwo_sb], dma=True)
            py_ring = Ring(S, [])
            py_ring.items = [(psum[4], PB[4]), (psum[5], PB[5])]
            oT2 = oTs.rearrange("h d t -> (h d) t")
            for t0 in range(0, SEQ, W):
                S.op("sp", lambda e, t0=t0: e.dma_start(out=xc, in_=xview(src, t0, W)), reads=xs_bufs(t0, W), writes=[bxc], dma=True)
                S.op("sp", lambda e, t0=t0: e.dma_start(out=oc, in_=oT2[:, t0:t0 + W].rearrange("(c p) t -> p c t", p=128)), reads=[b_oT], writes=[boc], dma=True)
                out_proj_residual(i, wo, bwo_sb, lambda k: oc[:, k, :], boc, xc, bxc, W, py_ring)
                S.op("sp", lambda e, t0=t0: e.dma_start(out=xview(dst, t0, W), in_=xc), reads=[bxc], writes=xs_bufs(t0, W), dma=True)
            cur_src[0] = dst

        def copy_phase(dst):
            S.barrier()
            AR.reset()
            src = x_src()
            xb = AR.alloc([128, KC, 512], F32)
            bx = Buf(S)
            for t0 in range(0, SEQ, 512):
                S.op("sp", lambda e, t0=t0: e.dma_start(out=xb, in_=xview(src, t0, 512)),
                     reads=xs_bufs(t0, 512), writes=[bx], dma=True)
                S.op("sp", lambda e, t0=t0: e.dma_start(out=xview(dst, t0, 512), in_=xb),
                     reads=[bx], writes=xs_bufs(t0, 512), dma=True)
            cur_src[0] = dst

        nphase = []
        for i in range(nlayers):
            if do_ffn:
                nphase.append(("ffn", i, 0, 0))
            if (i % 4) in mixers:
                nphase.append(("mix", i, i % 4, 1))
            if do_ffn:
                nphase.append(("ffn", i, 1, 2))
        if not nphase:
            nphase.append(("copy", 0, 0, 0))
        for pi, (kind, i, a, s) in enumerate(nphase):
            dst = outT if pi == len(nphase) - 1 else xs
            if kind == "ffn":
                ffn_phase(i, a, s, dst)
            elif kind == "copy":
                copy_phase(dst)
            else:
                if a == 0:
                    pool_phase(i, dst)
                elif a == 3:
                    conv_phase(i, dst)
                elif a == 2:
                    s5_phase(i, dst)
                else:
                    fox_phase(i, dst)
        S.barrier()
        S.emit(st)
    return nc


def _fm(v):
    return np.ascontiguousarray(np.asarray(v, np.float32).reshape(KC, 128).T)


def prep_inputs(inp, SEQ):
    f = lambda a: np.ascontiguousarray(np.asarray(a, dtype=np.float32))
    HALF = SEQ // 2
    x = f(inp["x"])
    c_all = f(inp["c"])
    shared = {}
    shared["ada_bT"] = np.ascontiguousarray(np.concatenate([f(inp["ada_b"])[i].reshape(72, 128).T for i in range(4)], axis=1))
    ng = f(inp["norm_g"])
    shared["norm_gT"] = np.ascontiguousarray(np.concatenate([_fm(ng[i, s]) for i in range(4) for s in range(3)], axis=1))
    shared["cTall"] = np.ascontiguousarray(c_all.reshape(4, KC, 128).transpose(2, 1, 0))
    shared["pool_sT"] = _fm(inp["pool_scale"][0])
    icnt = np.zeros((128, KC, 16), np.float32)
    for k in range(KC):
        w = 2 << (k // 2)
        icnt[:, k, :] = 1.0 / np.minimum(np.arange(16) + 1, w)
    shared["pool_icnt"] = icnt
    shared["fox_bf"] = f(inp["fox_b_f"])[0].reshape(16, 1)
    shared["fox_qg"] = np.ascontiguousarray(np.tile(f(inp["fox_q_gain"])[0], 2).reshape(128, 1))
    shared["fox_kg"] = np.ascontiguousarray(np.tile(f(inp["fox_k_gain"])[0], 2).reshape(128, 1))
    shared["s5_lre"] = np.ascontiguousarray(f(inp["s5_lam_re"])[0].reshape(32, 128).T)
    shared["s5_lim"] = np.ascontiguousarray(f(inp["s5_lam_im"])[0].reshape(32, 128).T)
    shared["s5_ldt"] = np.ascontiguousarray(np.repeat(f(inp["s5_log_dt"])[0], 64).reshape(32, 128).T)
    t16 = lambda a: np.ascontiguousarray(a.reshape(32, 128, 16).transpose(1, 0, 2))
    shared["s5_bre"] = t16(f(inp["s5_b_re"])[0].reshape(4096, 16))
    shared["s5_bim"] = t16(f(inp["s5_b_im"])[0].reshape(4096, 16))
    shared["s5_cre"] = t16(f(inp["s5_c_re"])[0].transpose(0, 2, 1).reshape(4096, 16))
    shared["s5_cim"] = t16(f(inp["s5_c_im"])[0].transpose(0, 2, 1).reshape(4096, 16))
    shared["s5_dT"] = _fm(inp["s5_d"][0])
    cw = f(inp["conv_w"])[0]
    shared["conv_wT"] = np.ascontiguousarray(np.concatenate([_fm(cw[j, 0]) for j in range(3)], axis=1))
    shared["ident"] = np.eye(128, dtype=np.float32)
    shared["tri"] = np.triu(np.ones((128, 128), np.float32))
    bo = np.zeros((128, 128), np.float32)
    bo[:64, :64] = 1
    bo[64:, 64:] = 1
    shared["bones"] = bo
    shared["iota"] = np.ascontiguousarray(np.tile(np.arange(256, dtype=np.float32), (128, 1)))
    mixw = np.concatenate([f(inp["fox_w_in"])[0], f(inp["fox_w_o"])[0], f(inp["s5_w_glu"])[0], f(inp["conv_w_in"])[0],
                           f(inp["conv_w_out"])[0], f(inp["pool_w"])[0].reshape(1024, 256)], axis=1)
    ada_w = inp["ada_w"]
    w_in = inp["ffn_w_in"]
    w_out = inp["ffn_w_out"]
    maps = []
    for c in range(NCORES):
        b, hf = c // 2, c % 2
        m = dict(shared)
        m["xTh"] = np.ascontiguousarray(x[b, hf * HALF:(hf + 1) * HALF].T)
        bs = np.zeros((128, 4), np.float32)
        bs[:, b] = 1.0
        m["bsel"] = bs
        m["ada_w_s"] = f(ada_w[:, :, c * 1152:(c + 1) * 1152])
        m["win_s"] = f(w_in[c // 2, c % 2])
        m["wout_s"] = f(w_out[c // 2, c % 2])
        m["mixw_s"] = np.ascontiguousarray(mixw[c * 128:(c + 1) * 128])
        maps.append(m)
    return maps


_NC_CACHE = {}


def assemble(res, B, SEQ):
    HALF = SEQ // 2
    out = np.empty((B, SEQ, D), np.float32)
    for b in range(B):
        o = res.results[2 * b]["outT"]
        for r in range(2):
            out[b, r * HALF:(r + 1) * HALF] = o[:, r].reshape(D, HALF).T
    return out


def kernel(**inputs):
    x = np.asarray(inputs["x"])
    B, SEQ, _ = x.shape
    key = ("full", SEQ)
    if key not in _NC_CACHE:
        _NC_CACHE[key] = build(SEQ)
    nc = _NC_CACHE[key]
    maps = prep_inputs(inputs, SEQ)
    res = run_bass_kernel_spmd(nc, maps, core_ids=list(range(NCORES)))
    return assemble(res, B, SEQ)
```

```python
import math
import numpy as np
from contextlib import ExitStack
import concourse.bass as bass
import concourse.mybir as mybir
from concourse.bass_utils import run_bass_kernel_spmd

F32 = mybir.dt.float32
BF16 = mybir.dt.bfloat16
I32 = mybir.dt.int32
AF = mybir.ActivationFunctionType
ALU = mybir.AluOpType

D = 1024
KC = 8
DFF = 2816
MC = 22
NG = 11
EPS = 1e-6
NCORES = 8


class Buf:
    __slots__ = ("name", "last_w", "rc", "rd")

    def __init__(self, S=None, name=""):
        self.name = name
        self.last_w = None
        self.rc = {}
        self.rd = []
        if S is not None:
            S.bufs.append(self)

    def reader_toks(self):
        return list(self.rc.values()) + self.rd

    def add_reader(self, tok):
        if tok[0] == 'c':
            self.rc[tok[1]] = tok
        else:
            self.rd.append(tok)
            if len(self.rd) > 24:
                self.rd = self.rd[-24:]

    def clear_readers(self):
        self.rc = {}
        self.rd = []


class Sched:
    NS = 8

    def __init__(self, nc):
        self.nc = nc
        self.names = ["pe", "dve", "act", "pool", "sp"]
        self.ops = {k: [] for k in self.names}
        self.cw = {}
        self.dw = {}
        self.kw = set()
        self.ndma = {k: 0 for k in self.names}
        self.targets = {k: set() for k in self.names}
        self.ncoll = 0
        self.bufs = []
        self.same_engine_sync = True

    def _need(self, eng, tok):
        if tok is None:
            return False
        if tok[0] == 'c':
            _, te, idx = tok
            if te == eng and (eng == 'pe' or not self.same_engine_sync):
                return False
            key = (eng, te)
            if self.cw.get(key, -1) >= idx:
                return False
            self.cw[key] = idx
            self.targets[te].add(idx)
            return True
        elif tok[0] == 'k':
            if (eng, tok[2]) in self.kw:
                return False
            self.kw.add((eng, tok[2]))
            return True
        else:
            _, q, di = tok
            key = (eng, q, di % self.NS)
            if self.dw.get(key, -1) >= di:
                return False
            self.dw[key] = di
            return True

    def op(self, eng, fn, reads=(), writes=(), dma=False, coll=False):
        deps = []
        for b in reads:
            if b.last_w is not None:
                deps.append(b.last_w)
        for b in writes:
            if b.last_w is not None:
                deps.append(b.last_w)
            deps.extend(b.reader_toks())
        waits = []
        seen = set()
        for t in deps:
            if t in seen:
                continue
            seen.add(t)
            if self._need(eng, t):
                waits.append(t)
        idx = len(self.ops[eng])
        if coll:
            tok = ('k', eng, self.ncoll)
            self.ncoll += 1
        elif dma:
            di = self.ndma[eng]
            self.ndma[eng] += 1
            tok = ('d', eng, di)
            if di >= self.NS:
                prev = ('d', eng, di - self.NS)
                if self._need(eng, prev):
                    waits.append(prev)
        else:
            tok = ('c', eng, idx)
        self.ops[eng].append((fn, waits, tok))
        for b in reads:
            b.add_reader(tok)
        for b in writes:
            b.last_w = tok
            b.clear_readers()
        return tok

    def barrier(self):
        toks = []
        seen = set()
        for b in self.bufs:
            for t in ([b.last_w] if b.last_w is not None else []) + b.reader_toks():
                if t not in seen:
                    seen.add(t)
                    toks.append(t)
            b.last_w = None
            b.clear_readers()
        toks.sort(key=lambda t: (t[0], t[1], -t[2]))
        for e in self.names:
            waits = [t for t in toks if self._need(e, t)]
            self.ops[e].append((None, waits, None))

    def emit(self, stack):
        nc = self.nc
        csem = {k: stack.enter_context(nc.semaphore("c_" + k)) for k in self.names}
        dsem = {}
        for k in self.names:
            if self.ndma[k] > 0:
                dsem[k] = [stack.enter_context(nc.semaphore("d_%s_%d" % (k, s))) for s in range(self.NS)]
        ksem = [stack.enter_context(nc.semaphore("k_%d" % i)) for i in range(self.ncoll)]
        cval = {}
        for k in self.names:
            n = 0
            for idx in sorted(self.targets[k]):
                n += 1
                cval[(k, idx)] = n

        def tokval(tok):
            if tok[0] == 'c':
                return csem[tok[1]], cval[(tok[1], tok[2])]
            if tok[0] == 'k':
                return ksem[tok[2]], 1
            _, q, di = tok
            return dsem[q][di % self.NS], 16 * (di // self.NS + 1)

        block = stack.enter_context(nc.Block())

        def make(k):
            def body(e):
                for (fn, waits, tok) in self.ops[k]:
                    for w in waits:
                        s, v = tokval(w)
                        e.wait_ge(s, v)
                    if fn is None:
                        continue
                    ins = fn(e)
                    if tok[0] == 'd':
                        s, v = tokval(tok)
                        ins.then_inc(s, 16)
                    elif tok[0] == 'k':
                        s, v = tokval(tok)
                        ins.then_inc(s)
                    elif tok[2] in self.targets[k]:
                        ins.then_inc(csem[k], 1)
            return body

        block.tensor(make("pe"))
        block.vector(make("dve"))
        block.scalar(make("act"))
        block.gpsimd(make("pool"))
        block.sync(make("sp"))


class Arena:
    def __init__(self, t, nwords):
        self.t = t
        self.n = nwords
        self.off = 0

    def reset(self):
        self.off = 0

    def alloc(self, shape, dt, parts=None):
        n = 1
        for s in shape[1:]:
            n *= s
        words = n if dt in (F32, I32) else (n + 1) // 2
        words = (words + 1) // 2 * 2
        P = shape[0]
        ap = self.t[0:P, self.off:self.off + words]
        self.off += words
        assert self.off <= self.n, ("arena overflow", self.off, self.n)
        if dt != F32:
            ap = ap.bitcast(dt)
        ap = ap[:, 0:n]
        if len(shape) == 3:
            ap = ap.rearrange("p (a b) -> p a b", a=shape[1])
        elif len(shape) == 4:
            ap = ap.rearrange("p (a b c) -> p a b c", a=shape[1], b=shape[2])
        return ap


class Ring:
    def __init__(self, S, aps):
        self.items = [(a, Buf(S)) for a in aps]
        self.i = 0

    def next(self):
        it = self.items[self.i % len(self.items)]
        self.i += 1
        return it


DEBUG = False
FOX_STAGE = 3
FOX_DBG = ''


def build(SEQ, mixers=(0, 1, 2, 3), nlayers=4, do_ffn=True):
    nc = bass.Bass("TRN2", target_bir_lowering=False)
    S = Sched(nc)

    def din(name, shape, dt=F32):
        return nc.dram_tensor(name, list(shape), dt, kind="ExternalInput").ap()

    HALF = SEQ // 2
    MIXC = 3088 + 1024 + 1024 + 3072 + 1024 + 256
    xTh_in = din("xTh", [KC * 128, HALF])
    cTall_in = din("cTall", [128, KC, 4])
    bsel_in = din("bsel", [128, 4])
    ada_w_s = din("ada_w_s", [4, D, 1152])
    ada_bT = din("ada_bT", [128, 288])
    norm_gT = din("norm_gT", [128, 96])
    win_s = din("win_s", [D, 2 * DFF])
    wout_s = din("wout_s", [DFF, D])
    mixw_s = din("mixw_s", [128, MIXC])
    pool_sT = din("pool_sT", [128, KC])
    pool_icnt = din("pool_icnt", [128, KC, 16])
    fox_bf = din("fox_bf", [16, 1])
    fox_qg = din("fox_qg", [128, 1])
    fox_kg = din("fox_kg", [128, 1])
    s5_lre = din("s5_lre", [128, 32])
    s5_lim = din("s5_lim", [128, 32])
    s5_ldt = din("s5_ldt", [128, 32])
    s5_bre = din("s5_bre", [128, 32, 16])
    s5_bim = din("s5_bim", [128, 32, 16])
    s5_cre = din("s5_cre", [128, 32, 16])
    s5_cim = din("s5_cim", [128, 32, 16])
    s5_dT = din("s5_dT", [128, KC])
    conv_wT = din("conv_wT", [128, 24])
    ident_in = din("ident", [128, 128])
    tri_in = din("tri", [128, 128])
    bones_in = din("bones", [128, 128])
    iota_in = din("iota", [128, 256])
    outT = nc.dram_tensor("outT", [KC, 2, 128, HALF], F32, kind="ExternalOutput").ap()
    dbg = nc.dram_tensor("dbg", [128, 4096], F32, kind="ExternalOutput").ap() if DEBUG else None

    def dscr(name, shape, dt):
        return nc.dram_tensor(name, list(shape), dt).ap()

    ALL8 = [list(range(8))]
    PAIRS = [[0, 1], [2, 3], [4, 5], [6, 7]]
    x_bounce = dscr("x_bounce", [KC * 128, HALF], F32)
    xs2d = dscr("xs", [2 * KC * 128, HALF], F32)
    xs = xs2d.rearrange("(k r p) t -> k r p t", r=2, k=KC)
    win_bo = dscr("win_bo", [2 * NG * 128, KC * 256], BF16)
    win_all = dscr("win_all", [8 * 2 * NG * 128, KC * 256], BF16)
    win_bf = win_all.rearrange("(r g p) (k c) -> r g p k c", r=8, g=2 * NG, k=KC)
    wout_bo = dscr("wout_bo", [KC * 128, MC * 128], BF16)
    wout_all = dscr("wout_all", [8 * KC * 128, MC * 128], BF16)
    wout_bf = wout_all.rearrange("(r n p) (m j) -> r n p m j", r=8, n=KC, m=MC)
    mixw_bo = dscr("mixw_bo", [128, MIXC], BF16)
    mixw_all = dscr("mixw_all", [D, MIXC], BF16)
    foxin_bf = mixw_all[:, 0:3088]
    foxo_bf = mixw_all[:, 3088:4112]
    glu_bf = mixw_all[:, 4112:5136]
    convin_bf = mixw_all[:, 5136:8208]
    convo_bf = mixw_all[:, 8208:9232]
    poolw_bf = mixw_all[:, 9232:9488].rearrange("(g a) b -> g a b", g=4)
    mod_bo = dscr("mod_bo", [128, 144], F32)
    mod_all = dscr("mod_all", [8 * 128, 144], F32)
    NKB = SEQ // 128
    NQB = SEQ // 512
    def dscr2(name, shape, dt):
        if DEBUG:
            return nc.dram_tensor(name, list(shape), dt, kind="ExternalOutput").ap()
        return dscr(name, shape, dt)

    qTa = dscr("qTa", [16, 66, SEQ], BF16)
    kTa = dscr("kTa", [16, 66, SEQ], BF16)
    vS = dscr("vS", [16, NKB, 128, 64], BF16)
    oTs = dscr("oTs", [16, 64, SEQ], BF16)
    cD = dscr("cD", [16, NQB], F32)

    b_xs = [Buf(S, "xs%d" % i) for i in range(max(1, SEQ // 256))]

    def xs_bufs(t0, n):
        return b_xs[t0 // 256:(t0 + n + 255) // 256]

    def xview(base, t0, n):
        r, tt = t0 // HALF, t0 % HALF
        return base[:, r, :, tt:tt + n].rearrange("k p t -> p k t")

    with ExitStack() as st:
        arena_t = st.enter_context(nc.sbuf_tensor("arena", [128, 46 * 1024], F32))
        AR = Arena(arena_t, 46 * 1024)
        cst_t = st.enter_context(nc.sbuf_tensor("cst", [128, 2048], F32))
        CST = Arena(cst_t, 2048)
        psum = [st.enter_context(nc.psum_tensor("ps%d" % i, [128, 512], F32)) for i in range(8)]
        PB = [Buf(S, "psum%d" % i) for i in range(8)]

        ident = CST.alloc([128, 128], F32)
        ones_bf = CST.alloc([128, 128], BF16)
        bones_bf = CST.alloc([128, 128], BF16)
        tri_bf = CST.alloc([128, 128], BF16)
        ones_f = CST.alloc([128, 64], F32)
        modT = CST.alloc([128, 288], F32)
        SPt = CST.alloc([128, 96], F32)
        GPt = CST.alloc([128, 96], F32)
        ngT = CST.alloc([128, 96], F32)
        condA = CST.alloc([128, KC, 4], F32)
        bsel = CST.alloc([128, 4], F32)
        b_c = Buf(S, "consts")
        b_mod = Buf(S, "mod")

        S.op("sp", lambda e: e.dma_start(out=ident, in_=ident_in), writes=[b_c], dma=True)
        S.op("sp", lambda e: e.dma_start(out=ngT, in_=norm_gT), writes=[b_c], dma=True)
        S.op("sp", lambda e: e.dma_start(out=condA, in_=cTall_in), writes=[b_c], dma=True)
        S.op("sp", lambda e: e.dma_start(out=bsel, in_=bsel_in), writes=[b_c], dma=True)
        S.op("dve", lambda e: e.memset(ones_bf, 1.0), writes=[b_c])
        S.op("dve", lambda e: e.memset(ones_f, 1.0), writes=[b_c])
        tmpc = AR.alloc([128, 128], F32)
        tmpc2 = AR.alloc([128, 128], F32)
        b_t = Buf(S)
        S.op("sp", lambda e: e.dma_start(out=tmpc, in_=tri_in), writes=[b_t], dma=True)
        S.op("dve", lambda e: e.tensor_copy(out=tri_bf, in_=tmpc), reads=[b_t], writes=[b_c])
        S.op("sp", lambda e: e.dma_start(out=tmpc2, in_=bones_in), writes=[b_t], dma=True)
        S.op("dve", lambda e: e.tensor_copy(out=bones_bf, in_=tmpc2), reads=[b_t], writes=[b_c])
        S.op("act", lambda e: e.activation(out=condA, in_=condA, func=AF.Silu), reads=[b_c], writes=[b_c])

        b_xbo = Buf(S, "xbo")
        S.op("sp", lambda e: e.dma_start(out=x_bounce, in_=xTh_in), writes=[b_xbo], dma=True)
        for k in range(KC):
            S.op("pool", lambda e, k=k: e.collective_compute("AllGather", ALU.bypass, replica_groups=PAIRS, ins=[x_bounce[k * 128:(k + 1) * 128, :]],
                                                             outs=[xs2d[k * 256:(k + 1) * 256, :]]), reads=[b_xbo], writes=b_xs, coll=True)

        b_wbo = Buf(S, "wbo")
        b_win = Buf(S, "win_all")
        b_wout = Buf(S, "wout_all")
        b_mixw_all = Buf(S, "mixw_all")
        win_bo_v = win_bo.rearrange("(g p) (k c) -> g p k c", g=2 * NG, k=KC)
        if do_ffn:
            for g in range(2 * NG):
                src = win_s[:, g * 256:(g + 1) * 256].rearrange("(k p) c -> p k c", p=128)
                S.op("pool", lambda e, g=g, src=src: e.dma_start(out=win_bo_v[g], in_=src), writes=[b_wbo], dma=True)
            S.op("pool", lambda e: e.collective_compute("AllGather", ALU.bypass, replica_groups=ALL8, ins=[win_bo], outs=[win_all]),
                 reads=[b_wbo], writes=[b_win], coll=True)
            wout_bo_v = wout_bo.rearrange("(n p) (m j) -> n p m j", n=KC, m=MC)
            b_wbo2 = Buf(S, "wbo2")
            for n in range(KC):
                src = wout_s[:, n * 128:(n + 1) * 128].rearrange("(m p) j -> p m j", p=128)
                S.op("pool", lambda e, n=n, src=src: e.dma_start(out=wout_bo_v[n], in_=src), writes=[b_wbo2], dma=True)
            S.op("pool", lambda e: e.collective_compute("AllGather", ALU.bypass, replica_groups=ALL8, ins=[wout_bo], outs=[wout_all]),
                 reads=[b_wbo2], writes=[b_wout], coll=True)
        b_wbo3 = Buf(S, "wbo3")
        S.op("pool", lambda e: e.dma_start(out=mixw_bo, in_=mixw_s), writes=[b_wbo3], dma=True)
        S.op("pool", lambda e: e.collective_compute("AllGather", ALU.bypass, replica_groups=ALL8, ins=[mixw_bo], outs=[mixw_all]),
             reads=[b_wbo3], writes=[b_mixw_all], coll=True)
        b_mixw = {k: b_mixw_all for k in ["pool", "foxin", "foxo", "glu", "convin", "convo"]}

        S.barrier()
        AR.reset()
        ada_ring = Ring(S, [AR.alloc([128, KC, 128], F32) for _ in range(4)])
        pmod = psum[7]
        for i in range(nlayers):
            for jj in range(9):
                at, ab = ada_ring.next()
                src = ada_w_s[i][:, jj * 128:(jj + 1) * 128].rearrange("(k p) c -> p k c", p=128)
                S.op("sp", lambda e, at=at, src=src: e.dma_start(out=at, in_=src), writes=[ab], dma=True)
                col = (i * 9 + jj) * 4
                for k in range(KC):
                    S.op("pe", lambda e, at=at, k=k, col=col: e.matmul(
                        pmod[:, col:col + 4], lhsT=at[:, k, :], rhs=condA[:, k, :],
                        start=(k == 0), stop=(k == KC - 1)), reads=[ab, b_c], writes=[PB[7]])
        mpart = AR.alloc([128, 144], F32)
        b_mp = Buf(S)
        if nlayers < 4:
            S.op("dve", lambda e: e.memset(mpart, 0.0), writes=[b_mp])
        S.op("dve", lambda e: e.tensor_copy(out=mpart[:, 0:nlayers * 36], in_=pmod[:, 0:nlayers * 36]), reads=[PB[7]], writes=[b_mp])
        b_mbo = Buf(S)
        b_mall = Buf(S)
        S.op("sp", lambda e: e.dma_start(out=mod_bo, in_=mpart), reads=[b_mp], writes=[b_mbo], dma=True)
        S.op("pool", lambda e: e.collective_compute("AllGather", ALU.bypass, replica_groups=ALL8, ins=[mod_bo], outs=[mod_all]),
             reads=[b_mbo], writes=[b_mall], coll=True)
        G = AR.alloc([128, 8, 144], F32)
        b_G = Buf(S)
        S.op("sp", lambda e: e.dma_start(out=G, in_=mod_all.rearrange("(r p) c -> p r c", p=128)), reads=[b_mall], writes=[b_G], dma=True)
        abT = AR.alloc([128, 288], F32)
        b_ab = Buf(S)
        S.op("sp", lambda e: e.dma_start(out=abT, in_=ada_bT), writes=[b_ab], dma=True)
        G5 = G.rearrange("p r (i j b) -> p r i j b", i=4, j=9)
        for i in range(4):
            mo = modT[:, i * 72:(i + 1) * 72].rearrange("p (r j) -> p r j", r=8)
            for b in range(4):
                gv = G5[:, :, i, :, b]
                if b == 0:
                    S.op("dve", lambda e, mo=mo, gv=gv, b=b: e.tensor_scalar(out=mo, in0=gv, scalar1=bsel[:, b:b + 1], scalar2=None, op0=ALU.mult),
                         reads=[b_G, b_c], writes=[b_mod])
                else:
                    S.op("dve", lambda e, mo=mo, gv=gv, b=b: e.scalar_tensor_tensor(out=mo, in0=gv, scalar=bsel[:, b:b + 1], in1=mo,
                                                                                 op0=ALU.mult, op1=ALU.add), reads=[b_G, b_c, b_mod], writes=[b_mod])
        S.op("dve", lambda e: e.tensor_tensor(out=modT, in0=modT, in1=abT, op=ALU.add), reads=[b_mod, b_ab], writes=[b_mod])
        for i in range(nlayers):
            for s in range(3):
                c0 = (i * 3 + s) * 8
                sc = i * 72 + (s * 3 + 1) * 8
                gt = i * 72 + (s * 3 + 2) * 8
                S.op("dve", lambda e, c0=c0, sc=sc: e.scalar_tensor_tensor(
                    out=SPt[:, c0:c0 + 8], in0=modT[:, sc:sc + 8], scalar=1.0, in1=ngT[:, c0:c0 + 8],
                    op0=ALU.add, op1=ALU.mult), reads=[b_mod, b_c], writes=[b_mod])
                half = 1.0 if s == 1 else 0.5
                S.op("dve", lambda e, c0=c0, gt=gt, half=half: e.tensor_scalar(
                    out=GPt[:, c0:c0 + 8], in0=modT[:, gt:gt + 8], scalar1=1.0, scalar2=half,
                    op0=ALU.add, op1=ALU.mult), reads=[b_mod], writes=[b_mod])

        def shift_col(i, s, k):
            return modT[:, i * 72 + (s * 3) * 8 + k: i * 72 + (s * 3) * 8 + k + 1]

        def sp_col(i, s, k):
            c = (i * 3 + s) * 8 + k
            return SPt[:, c:c + 1]

        def gp_col(i, s, k):
            c = (i * 3 + s) * 8 + k
            return GPt[:, c:c + 1]

        def prenorm(i, s, xb, bx, W, sq, bsq, rs, brs, tmp_ring, outs):
            pss, bps = psum[6], PB[6]
            for k in range(KC):
                S.op("pool", lambda e, k=k: e.tensor_tensor(out=sq[:, k, 0:W], in0=xb[:, k, 0:W], in1=xb[:, k, 0:W], op=ALU.mult),
                     reads=[bx], writes=[bsq])
            for k in range(KC):
                S.op("pe", lambda e, k=k: e.matmul(pss[:, 0:W], lhsT=ones_bf, rhs=sq[:, k, 0:W], start=(k == 0), stop=(k == KC - 1)),
                     reads=[bsq, b_c], writes=[bps])
            S.op("act", lambda e: e.activation(out=rs[:, 0:W], in_=pss[:, 0:W], func=AF.Sqrt, bias=EPS, scale=1.0 / D),
                 reads=[bps], writes=[brs])
            S.op("dve", lambda e: e.reciprocal(out=rs[:, 0:W], in_=rs[:, 0:W]), reads=[brs], writes=[brs])
            for k in range(KC):
                tt, tb = tmp_ring.next()
                S.op("dve", lambda e, k=k, tt=tt: e.tensor_tensor(out=tt[:, 0:W], in0=xb[:, k, 0:W], in1=rs[:, 0:W], op=ALU.mult),
                     reads=[bx, brs], writes=[tb])
                for (o, bo) in outs:
                    S.op("act", lambda e, k=k, tt=tt, o=o: e.activation(
                        out=o(k), in_=tt[:, 0:W], func=AF.Identity, bias=shift_col(i, s, k), scale=sp_col(i, s, k)),
                        reads=[tb, b_mod], writes=[bo])

        cur_src = [xs]

        def x_src():
            return cur_src[0]

        def ffn_phase(i, f, s, dst):
            S.barrier()
            AR.reset()
            TB = 1024 if HALF % 1024 == 0 else 512
            NSB = TB // 512
            src = x_src()
            xb_ring = Ring(S, [AR.alloc([128, KC, TB], F32) for _ in range(2)])
            h = AR.alloc([128, KC, TB], BF16)
            bh = Buf(S)
            hid = AR.alloc([128, MC, TB], BF16)
            bhid = Buf(S)
            sq = AR.alloc([128, KC, 512], BF16)
            bsq = Buf(S)
            rs = AR.alloc([128, 512], F32)
            brs = Buf(S)
            tmp_ring = Ring(S, [AR.alloc([128, 512], F32) for _ in range(2)])
            sg_ring = Ring(S, [AR.alloc([128, 512], F32) for _ in range(2)])
            wg_ring = Ring(S, [AR.alloc([128, KC, 256], BF16) for _ in range(2)])
            wu_ring = Ring(S, [AR.alloc([128, KC, 256], BF16) for _ in range(2)])
            wo_ring = Ring(S, [AR.alloc([128, MC, 128], BF16) for _ in range(2)])
            pg_ring = Ring(S, [])
            pg_ring.items = [(psum[0], PB[0]), (psum[1], PB[1])]
            pu_ring = Ring(S, [])
            pu_ring.items = [(psum[2], PB[2]), (psum[3], PB[3])]
            py_ring = Ring(S, [])
            py_ring.items = [(psum[4], PB[4]), (psum[5], PB[5])]
            bwi = b_win
            bwo = b_wout
            fi = i * 2 + f
            blocks = list(range(0, SEQ, TB))

            def load_x(t0):
                xb_, bx_ = xb_ring.next()
                S.op("sp", lambda e, xb_=xb_, t0=t0: e.dma_start(out=xb_, in_=xview(src, t0, TB)),
                     reads=xs_bufs(t0, TB), writes=[bx_], dma=True)
                return xb_, bx_

            nxt = load_x(blocks[0])
            for bi, t0 in enumerate(blocks):
                xb, bx = nxt
                for sb in range(NSB):
                    c0 = sb * 512
                    xv = xb[:, :, c0:c0 + 512]
                    prenorm(i, s, xv, bx, 512, sq, bsq, rs, brs, tmp_ring,
                            [(lambda k, c0=c0: h[:, k, c0:c0 + 512], bh)])
                for g in range(NG):
                    if g == 2 and bi + 1 < len(blocks):
                        nxt = load_x(blocks[bi + 1])
                    wg, bwg = wg_ring.next()
                    wu, bwu = wu_ring.next()
                    S.op("sp", lambda e, wg=wg, g=g: e.dma_start(out=wg, in_=win_bf[fi, g]), reads=[bwi], writes=[bwg], dma=True)
                    S.op("sp", lambda e, wu=wu, g=g: e.dma_start(out=wu, in_=win_bf[fi, NG + g]), reads=[bwi], writes=[bwu], dma=True)
                    for mm in range(2):
                        m = g * 2 + mm
                        for sb in range(NSB):
                            c0 = sb * 512
                            pg, bpg = pg_ring.next()
                            pu, bpu = pu_ring.next()
                            for k in range(KC):
                                S.op("pe", lambda e, pg=pg, wg=wg, mm=mm, k=k, c0=c0: e.matmul(
                                    pg[:, :], lhsT=wg[:, k, mm * 128:(mm + 1) * 128], rhs=h[:, k, c0:c0 + 512],
                                    start=(k == 0), stop=(k == KC - 1)), reads=[bwg, bh], writes=[bpg])
                            for k in range(KC):
                                S.op("pe", lambda e, pu=pu, wu=wu, mm=mm, k=k, c0=c0: e.matmul(
                                    pu[:, :], lhsT=wu[:, k, mm * 128:(mm + 1) * 128], rhs=h[:, k, c0:c0 + 512],
                                    start=(k == 0), stop=(k == KC - 1)), reads=[bwu, bh], writes=[bpu])
                            sg, bsg = sg_ring.next()
                            S.op("act", lambda e, sg=sg, pg=pg: e.activation(out=sg, in_=pg[:, :], func=AF.Silu), reads=[bpg], writes=[bsg])
                            S.op("dve", lambda e, sg=sg, pu=pu, m=m, c0=c0: e.tensor_tensor(
                                out=hid[:, m, c0:c0 + 512], in0=sg, in1=pu[:, :], op=ALU.mult), reads=[bsg, bpu], writes=[bhid])
                for n in range(KC):
                    wo, bwo_t = wo_ring.next()
                    S.op("sp", lambda e, wo=wo, n=n: e.dma_start(out=wo, in_=wout_bf[fi, n]), reads=[bwo], writes=[bwo_t], dma=True)
                    for sb in range(NSB):
                        c0 = sb * 512
                        py, bpy = py_ring.next()
                        for m in range(MC):
                            S.op("pe", lambda e, py=py, wo=wo, m=m, c0=c0: e.matmul(
                                py[:, :], lhsT=wo[:, m, :], rhs=hid[:, m, c0:c0 + 512], start=(m == 0), stop=(m == MC - 1)),
                                reads=[bwo_t, bhid], writes=[bpy])
                        S.op("dve", lambda e, py=py, n=n, c0=c0, xb=xb: e.scalar_tensor_tensor(
                            out=xb[:, n, c0:c0 + 512], in0=py[:, :], scalar=gp_col(i, s, n), in1=xb[:, n, c0:c0 + 512],
                            op0=ALU.mult, op1=ALU.add), reads=[bpy, b_mod, bx], writes=[bx])
                S.op("sp", lambda e, xb=xb, t0=t0: e.dma_start(out=xview(dst, t0, TB), in_=xb),
                     reads=[bx], writes=xs_bufs(t0, TB), dma=True)
            cur_src[0] = dst

        def out_proj_residual(i, wo_sb, bwo_sb, rhs_fn, brhs, xb, bx, W, py_ring):
            for n in range(KC):
                py, bpy = py_ring.next()
                for k in range(KC):
                    S.op("pe", lambda e, py=py, n=n, k=k: e.matmul(
                        py[:, 0:W], lhsT=wo_sb[:, k, n * 128:(n + 1) * 128], rhs=rhs_fn(k), start=(k == 0), stop=(k == KC - 1)),
                        reads=[bwo_sb, brhs], writes=[bpy])
                S.op("dve", lambda e, py=py, n=n: e.scalar_tensor_tensor(
                    out=xb[:, n, 0:W], in0=py[:, 0:W], scalar=gp_col(i, 1, n), in1=xb[:, n, 0:W],
                    op0=ALU.mult, op1=ALU.add), reads=[bpy, b_mod, bx], writes=[bx])

        def conv_phase(i, dst):
            S.barrier()
            AR.reset()
            W = 512
            src = x_src()
            xb = AR.alloc([128, KC, W], F32)
            bx = Buf(S)
            h = AR.alloc([128, KC, W], BF16)
            bh = Buf(S)
            sq = AR.alloc([128, KC, 512], BF16)
            bsq = Buf(S)
            rs = AR.alloc([128, 512], F32)
            brs = Buf(S)
            tmp_ring = Ring(S, [AR.alloc([128, 512], F32) for _ in range(2)])
            cz = AR.alloc([128, KC, W + 2], F32)
            bcz = Buf(S)
            csb_ring = Ring(S, [AR.alloc([128, W], F32) for _ in range(2)])
            t1_ring = Ring(S, [AR.alloc([128, W], F32) for _ in range(2)])
            mmb = AR.alloc([128, KC, W], BF16)
            bmm = Buf(S)
            win = AR.alloc([128, KC, 3 * D], BF16)
            bwin_sb = Buf(S)
            wo = AR.alloc([128, KC, D], BF16)
            bwo_sb = Buf(S)
            cw = AR.alloc([128, 24], F32)
            bcw = Buf(S)
            S.op("sp", lambda e: e.dma_start(out=cw, in_=conv_wT), writes=[bcw], dma=True)
            for k in range(KC):
                S.op("sp", lambda e, k=k: e.dma_start(out=win[:, k, :], in_=convin_bf[k * 128:(k + 1) * 128, :]),
                     reads=[b_mixw["convin"]], writes=[bwin_sb], dma=True)
            S.op("sp", lambda e: e.dma_start(out=wo, in_=convo_bf.rearrange("(k p) n -> p k n", p=128)),
                 reads=[b_mixw["convo"]], writes=[bwo_sb], dma=True)
            S.op("dve", lambda e: e.memset(cz, 0.0), writes=[bcz])
            pa_ring = Ring(S, [])
            pa_ring.items = [(psum[0], PB[0]), (psum[1], PB[1])]
            pb_ring = Ring(S, [])
            pb_ring.items = [(psum[2], PB[2]), (psum[3], PB[3])]
            py_ring = Ring(S, [])
            py_ring.items = [(psum[4], PB[4]), (psum[5], PB[5])]
            for t0 in range(0, SEQ, W):
                S.op("sp", lambda e, t0=t0: e.dma_start(out=xb, in_=xview(src, t0, W)),
                     reads=xs_bufs(t0, W), writes=[bx], dma=True)
                prenorm(i, 1, xb, bx, W, sq, bsq, rs, brs, tmp_ring, [(lambda k: h[:, k, :], bh)])
                for k in range(KC):
                    pc, bpc = pa_ring.next()
                    pz, bpz = pb_ring.next()
                    for kk in range(KC):
                        S.op("pe", lambda e, pc=pc, k=k, kk=kk: e.matmul(
                            pc[:, 0:W], lhsT=win[:, kk, D + k * 128:D + (k + 1) * 128], rhs=h[:, kk, :],
                            start=(kk == 0), stop=(kk == KC - 1)), reads=[bwin_sb, bh], writes=[bpc])
                    for kk in range(KC):
                        S.op("pe", lambda e, pz=pz, k=k, kk=kk: e.matmul(
                            pz[:, 0:W], lhsT=win[:, kk, 2 * D + k * 128:2 * D + (k + 1) * 128], rhs=h[:, kk, :],
                            start=(kk == 0), stop=(kk == KC - 1)), reads=[bwin_sb, bh], writes=[bpz])
                    cs, bcs = csb_ring.next()
                    S.op("act", lambda e, cs=cs, pc=pc: e.activation(out=cs, in_=pc[:, 0:W], func=AF.Identity), reads=[bpc], writes=[bcs])
                    S.op("dve", lambda e, cs=cs, pz=pz, k=k: e.tensor_tensor(out=cz[:, k, 2:2 + W], in0=cs, in1=pz[:, 0:W], op=ALU.mult),
                         reads=[bcs, bpz], writes=[bcz])
                    t1, bt1 = t1_ring.next()
                    S.op("dve", lambda e, t1=t1, k=k: e.tensor_scalar(out=t1, in0=cz[:, k, 2:2 + W], scalar1=cw[:, 16 + k:17 + k], scalar2=None,
                                                                      op0=ALU.mult), reads=[bcz, bcw], writes=[bt1])
                    S.op("dve", lambda e, t1=t1, k=k: e.scalar_tensor_tensor(out=t1, in0=cz[:, k, 1:1 + W], scalar=cw[:, 8 + k:9 + k], in1=t1,
                                                                             op0=ALU.mult, op1=ALU.add), reads=[bcz, bcw, bt1], writes=[bt1])
                    S.op("dve", lambda e, t1=t1, k=k: e.scalar_tensor_tensor(out=t1, in0=cz[:, k, 0:W], scalar=cw[:, k:k + 1], in1=t1,
                                                                             op0=ALU.mult, op1=ALU.add), reads=[bcz, bcw, bt1], writes=[bt1])
                    pbg, bpbg = pa_ring.next()
                    for kk in range(KC):
                        S.op("pe", lambda e, pbg=pbg, k=k, kk=kk: e.matmul(
                            pbg[:, 0:W], lhsT=win[:, kk, k * 128:(k + 1) * 128], rhs=h[:, kk, :],
                            start=(kk == 0), stop=(kk == KC - 1)), reads=[bwin_sb, bh], writes=[bpbg])
                    S.op("dve", lambda e, t1=t1, pbg=pbg, k=k: e.tensor_tensor(out=mmb[:, k, :], in0=t1, in1=pbg[:, 0:W], op=ALU.mult),
                         reads=[bt1, bpbg], writes=[bmm])
                    S.op("pool", lambda e, k=k: e.tensor_copy(out=cz[:, k, 0:2], in_=cz[:, k, W:W + 2]), reads=[bcz], writes=[bcz])
                out_proj_residual(i, wo, bwo_sb, lambda k: mmb[:, k, :], bmm, xb, bx, W, py_ring)
                S.op("sp", lambda e, t0=t0: e.dma_start(out=xview(dst, t0, W), in_=xb),
                     reads=[bx], writes=xs_bufs(t0, W), dma=True)
            cur_src[0] = dst

        def pool_phase(i, dst):
            S.barrier()
            AR.reset()
            W = 512
            src = x_src()
            xb = AR.alloc([128, KC, W], F32)
            bx = Buf(S)
            hf = AR.alloc([128, KC, W + 16], F32)
            bhf = Buf(S)
            sq = AR.alloc([128, KC, 512], BF16)
            bsq = Buf(S)
            rs = AR.alloc([128, 512], F32)
            brs = Buf(S)
            tmp_ring = Ring(S, [AR.alloc([128, 512], F32) for _ in range(2)])
            sa_ring = Ring(S, [AR.alloc([128, W + 16], F32) for _ in range(2)])
            sb_ring = Ring(S, [AR.alloc([128, W + 16], F32) for _ in range(2)])
            pl = AR.alloc([128, KC, W], BF16)
            bpl = Buf(S)
            pw = AR.alloc([128, 4, 2, 256], BF16)
            bpw = Buf(S)
            icnt = AR.alloc([128, KC, 16], F32)
            psT = AR.alloc([128, KC], F32)
            gps = AR.alloc([128, KC], F32)
            bcn = Buf(S)
            S.op("sp", lambda e: e.dma_start(out=icnt, in_=pool_icnt), writes=[bcn], dma=True)
            S.op("sp", lambda e: e.dma_start(out=psT, in_=pool_sT), writes=[bcn], dma=True)
            c0g = (i * 3 + 1) * 8
            S.op("dve", lambda e: e.tensor_tensor(out=gps, in0=psT, in1=GPt[:, c0g:c0g + 8], op=ALU.mult), reads=[bcn, b_mod], writes=[bcn])
            S.op("sp", lambda e: e.dma_start(out=pw, in_=poolw_bf.rearrange("g (c p) d -> p g c d", p=128)),
                 reads=[b_mixw["pool"]], writes=[bpw], dma=True)
            S.op("dve", lambda e: e.memset(hf, 0.0), writes=[bhf])
            py_ring = Ring(S, [])
            py_ring.items = [(psum[4], PB[4]), (psum[5], PB[5])]
            for t0 in range(0, SEQ, W):
                S.op("sp", lambda e, t0=t0: e.dma_start(out=xb, in_=xview(src, t0, W)),
                     reads=xs_bufs(t0, W), writes=[bx], dma=True)
                prenorm(i, 1, xb, bx, W, sq, bsq, rs, brs, tmp_ring, [(lambda k: hf[:, k, 16:16 + W], bhf)])
                for k in range(KC):
                    gi = k // 2
                    cur = hf[:, k, :]
                    bcur = bhf
                    for j in range(gi + 1):
                        sp_ = 1 << j
                        lo = 2 * sp_ - 1
                        ring = sa_ring if j % 2 == 0 else sb_ring
                        nx, bnx = ring.next()
                        eng = "pool" if (j % 2 == 0) else "dve"
                        S.op(eng, lambda e, nx=nx, cur=cur, lo=lo, sp_=sp_: e.tensor_tensor(
                            out=nx[:, lo:W + 16], in0=cur[:, lo:W + 16], in1=cur[:, lo - sp_:W + 16 - sp_], op=ALU.add),
                            reads=[bcur], writes=[bnx])
                        cur, bcur = nx, bnx
                    w = 2 << gi
                    S.op("dve", lambda e, cur=cur, k=k, w=w: e.scalar_tensor_tensor(
                        out=pl[:, k, :], in0=cur[:, 16:16 + W], scalar=1.0 / w, in1=hf[:, k, 16:16 + W],
                        op0=ALU.mult, op1=ALU.subtract), reads=[bcur, bhf], writes=[bpl])
                    if t0 == 0:
                        tt, tb = tmp_ring.next()
                        S.op("dve", lambda e, cur=cur, k=k, tt=tt: e.tensor_tensor(out=tt[:, 0:16], in0=cur[:, 16:32], in1=icnt[:, k, :], op=ALU.mult),
                             reads=[bcur, bcn], writes=[tb])
                        S.op("dve", lambda e, k=k, tt=tt: e.tensor_tensor(out=pl[:, k, 0:16], in0=tt[:, 0:16], in1=hf[:, k, 16:32], op=ALU.subtract),
                             reads=[tb, bhf], writes=[bpl])
                for n in range(KC):
                    gi, dd = n // 2, n % 2
                    py, bpy = py_ring.next()
                    for cc in range(2):
                        S.op("pe", lambda e, py=py, gi=gi, dd=dd, cc=cc: e.matmul(
                            py[:, 0:W], lhsT=pw[:, gi, cc, dd * 128:(dd + 1) * 128], rhs=pl[:, 2 * gi + cc, :],
                            start=(cc == 0), stop=(cc == 1)), reads=[bpw, bpl], writes=[bpy])
                    S.op("dve", lambda e, py=py, n=n: e.scalar_tensor_tensor(
                        out=xb[:, n, :], in0=py[:, 0:W], scalar=gps[:, n:n + 1], in1=xb[:, n, :],
                        op0=ALU.mult, op1=ALU.add), reads=[bpy, bcn, bx], writes=[bx])
                for k in range(KC):
                    S.op("pool", lambda e, k=k: e.tensor_copy(out=hf[:, k, 0:16], in_=hf[:, k, W:W + 16]), reads=[bhf], writes=[bhf])
                S.op("sp", lambda e, t0=t0: e.dma_start(out=xview(dst, t0, W), in_=xb),
                     reads=[bx], writes=xs_bufs(t0, W), dma=True)
            cur_src[0] = dst

        def s5_phase(i, dst):
            S.barrier()
            AR.reset()
            T = 256
            src = x_src()
            TWO_PI = 6.283185
            sm = lambda: AR.alloc([128, 32], F32)
            lre, lim, ldt, dtt, ard, aid, mag, sn, cs_, kre, kim = [sm() for _ in range(11)]
            cT_, sT_, Xtr, Xti, INr, INi, t32a, t32b, t32c, t32d = [sm() for _ in range(10)]
            b_p = Buf(S, "s5pre")
            BreT = AR.alloc([128, 32, 128], BF16)
            BimT = AR.alloc([128, 32, 128], BF16)
            CreT = AR.alloc([128, 32, 128], BF16)
            CimTn = AR.alloc([128, 32, 128], BF16)
            b_W = Buf(S, "s5W")
            Ec = AR.alloc([128, 32, T], F32)
            Es = AR.alloc([128, 32, T], F32)
            b_E = Buf(S, "s5E")
            wglu = AR.alloc([128, KC, D], BF16)
            b_wg = Buf(S)
            dsk = AR.alloc([128, KC], F32)
            iot = AR.alloc([128, T], F32)
            mark = AR.off
            bre = AR.alloc([128, 32, 16], F32)
            bim = AR.alloc([128, 32, 16], F32)
            cre = AR.alloc([128, 32, 16], F32)
            cim = AR.alloc([128, 32, 16], F32)
            bbr = AR.alloc([128, 32, 16], F32)
            bbi = AR.alloc([128, 32, 16], F32)
            tb1 = AR.alloc([128, 32, 16], F32)
            i32t = AR.alloc([128, T], I32)
            fA = AR.alloc([128, T], F32)
            fB = AR.alloc([128, T], F32)
            fC = AR.alloc([128, T], F32)
            ms_ring = Ring(S, [AR.alloc([128, 128], F32) for _ in range(4)])
            for (dst_ap, src_ap) in [(lre, s5_lre), (lim, s5_lim), (ldt, s5_ldt), (bre, s5_bre), (bim, s5_bim), (cre, s5_cre),
                                     (cim, s5_cim), (dsk, s5_dT), (iot, iota_in[:, 0:T])]:
                S.op("sp", lambda e, a=dst_ap, b=src_ap: e.dma_start(out=a, in_=b), writes=[b_p], dma=True)
            S.op("sp", lambda e: e.dma_start(out=wglu, in_=glu_bf.rearrange("(k p) n -> p k n", p=128)), reads=[b_mixw["glu"]], writes=[b_wg], dma=True)

            def P(eng, fn, extra_r=(), extra_w=()):
                S.op(eng, fn, reads=[b_p] + list(extra_r), writes=[b_p] + list(extra_w))

            def sincos(theta, n, out_s, out_c):
                a, b_, c_, ii = fA[:, 0:n], fB[:, 0:n], fC[:, 0:n], i32t[:, 0:n]
                P("dve", lambda e: e.tensor_scalar(out=a, in0=theta, scalar1=1.0 / (2 * math.pi), scalar2=None, op0=ALU.mult))
                P("dve", lambda e: e.tensor_copy(out=ii, in_=a))
                P("dve", lambda e: e.tensor_copy(out=b_, in_=ii))
                P("dve", lambda e: e.tensor_tensor(out=a, in0=a, in1=b_, op=ALU.subtract))
                P("dve", lambda e: e.tensor_scalar(out=b_, in0=a, scalar1=0.5, scalar2=None, op0=ALU.is_gt))
                P("dve", lambda e: e.tensor_tensor(out=a, in0=a, in1=b_, op=ALU.subtract))
                P("dve", lambda e: e.tensor_scalar(out=b_, in0=a, scalar1=-0.5, scalar2=None, op0=ALU.is_lt))
                P("dve", lambda e: e.tensor_tensor(out=a, in0=a, in1=b_, op=ALU.add))
                P("act", lambda e: e.activation(out=out_s, in_=a, func=AF.Sin, scale=TWO_PI))
                P("dve", lambda e: e.tensor_scalar(out=c_, in0=a, scalar1=0.25, scalar2=None, op0=ALU.add))
                P("dve", lambda e: e.tensor_scalar(out=b_, in0=c_, scalar1=0.5, scalar2=None, op0=ALU.is_gt))
                P("dve", lambda e: e.tensor_tensor(out=c_, in0=c_, in1=b_, op=ALU.subtract))
                P("act", lambda e: e.activation(out=out_c, in_=c_, func=AF.Sin, scale=TWO_PI))

            tt = lambda o, a, b, op: P("dve", lambda e: e.tensor_tensor(out=o, in0=a, in1=b, op=op))
            P("act", lambda e: e.activation(out=dtt, in_=ldt, func=AF.Exp))
            tt(ard, lre, dtt, ALU.mult)
            tt(aid, lim, dtt, ALU.mult)
            P("act", lambda e: e.activation(out=mag, in_=ard, func=AF.Exp))
            sincos(aid, 32, sn, cs_)
            P("dve", lambda e: e.tensor_scalar(out=t32a, in0=aid, scalar1=float(T), scalar2=None, op0=ALU.mult))
            sincos(t32a, 32, sT_, cT_)
            tt(t32a, mag, cs_, ALU.mult)
            tt(t32b, mag, sn, ALU.mult)
            P("dve", lambda e: e.tensor_scalar(out=t32a, in0=t32a, scalar1=-1.0, scalar2=None, op0=ALU.add))
            tt(t32c, lre, lre, ALU.mult)
            tt(t32d, lim, lim, ALU.mult)
            tt(t32c, t32c, t32d, ALU.add)
            P("dve", lambda e: e.reciprocal(out=t32c, in_=t32c))
            tt(kre, t32a, lre, ALU.mult)
            tt(t32d, t32b, lim, ALU.mult)
            tt(kre, kre, t32d, ALU.add)
            tt(kre, kre, t32c, ALU.mult)
            tt(kim, t32b, lre, ALU.mult)
            tt(t32d, t32a, lim, ALU.mult)
            tt(kim, kim, t32d, ALU.subtract)
            tt(kim, kim, t32c, ALU.mult)
            bc = lambda a: a.unsqueeze(2).to_broadcast([128, 32, 16])
            tt(bbr, bre, bc(kre), ALU.mult)
            tt(tb1, bim, bc(kim), ALU.mult)
            tt(bbr, bbr, tb1, ALU.subtract)
            tt(bbi, bim, bc(kre), ALU.mult)
            tt(tb1, bre, bc(kim), ALU.mult)
            tt(bbi, bbi, tb1, ALU.add)
            P("dve", lambda e: e.memset(CreT, 0.0), extra_w=[b_W])
            P("dve", lambda e: e.memset(CimTn, 0.0), extra_w=[b_W])
            P("dve", lambda e: e.memset(Xtr, 0.0))
            P("dve", lambda e: e.memset(Xti, 0.0))
            P("dve", lambda e: e.memset(INr, 0.0))
            P("dve", lambda e: e.memset(INi, 0.0))
            ptr_ring = Ring(S, [])
            ptr_ring.items = [(psum[0], PB[0]), (psum[1], PB[1])]
            def tile_pre(ti):
                for (bb, dstT) in [(bbr, BreT), (bbi, BimT)]:
                    ms, bms = ms_ring.next()
                    S.op("pool", lambda e, ms=ms: e.memset(ms, 0.0), writes=[bms])
                    for gl in range(2):
                        c0 = ((ti % 4) * 2 + gl) * 16
                        S.op("pool", lambda e, ms=ms, gl=gl, c0=c0, bb=bb: e.tensor_copy(
                            out=ms[gl * 64:(gl + 1) * 64, c0:c0 + 16], in_=bb[gl * 64:(gl + 1) * 64, ti, :]), reads=[b_p], writes=[bms])
                    pt, bpt = ptr_ring.next()
                    S.op("pe", lambda e, pt=pt, ms=ms: e.transpose(out=pt[:, 0:128], in_=ms, identity=ident), reads=[bms, b_c], writes=[bpt])
                    S.op("act", lambda e, pt=pt, dstT=dstT: e.activation(out=dstT[:, ti, :], in_=pt[:, 0:128], func=AF.Identity), reads=[bpt], writes=[b_W])
                for gl in range(2):
                    c0 = ((ti % 4) * 2 + gl) * 16
                    S.op("dve", lambda e, gl=gl, c0=c0: e.tensor_copy(out=CreT[gl * 64:(gl + 1) * 64, ti, c0:c0 + 16],
                                                                      in_=cre[gl * 64:(gl + 1) * 64, ti, :]), reads=[b_p], writes=[b_W])
                    S.op("dve", lambda e, gl=gl, c0=c0: e.tensor_scalar(out=CimTn[gl * 64:(gl + 1) * 64, ti, c0:c0 + 16],
                                                                        in0=cim[gl * 64:(gl + 1) * 64, ti, :], scalar1=-1.0, scalar2=None, op0=ALU.mult),
                         reads=[b_p], writes=[b_W])
                P("dve", lambda e: e.tensor_scalar(out=fC[:, 0:T], in0=iot, scalar1=aid[:, ti:ti + 1], scalar2=None, op0=ALU.mult))
                P("dve", lambda e: e.tensor_copy(out=Ec[:, ti, :], in_=fC[:, 0:T]), extra_w=[b_E])
                sincos(Ec[:, ti, :], T, Es[:, ti, :], Ec[:, ti, :])
            for ti_ in range(32):
                tile_pre(ti_)
            if DEBUG:
                b_dbg = Buf(S)
                items = [(mag, 32), (sn, 32), (cs_, 32), (kre, 32), (kim, 32), (cT_, 32), (sT_, 32), (bbr.rearrange("p a b -> p (a b)"), 512),
                         (bbi.rearrange("p a b -> p (a b)"), 512), (Ec[:, 5, :], T), (Es[:, 5, :], T), (aid, 32)]
                o = 0
                for (a, n) in items:
                    S.op("sp", lambda e, a=a, o=o, n=n: e.dma_start(out=dbg[:, o:o + n], in_=a), reads=[b_p, b_E], writes=[b_dbg], dma=True)
                    o += n
            S.barrier()
            AR.off = mark
            xb = AR.alloc([128, KC, T], F32)
            bx = Buf(S)
            hf = AR.alloc([128, KC, T], F32)
            bhf = Buf(S)
            hb = AR.alloc([128, KC, T], BF16)
            bhb = Buf(S)
            gf = AR.alloc([128, KC, T], F32)
            bgf = Buf(S)
            gb = AR.alloc([128, KC, T], BF16)
            bgb = Buf(S)
            sq = AR.alloc([128, KC, T], BF16)
            bsq = Buf(S)
            rs = AR.alloc([128, T], F32)
            brs = Buf(S)
            tmp_ring = Ring(S, [AR.alloc([128, T], F32) for _ in range(2)])
            vr_ring = Ring(S, [AR.alloc([128, T], F32) for _ in range(2)])
            vi_ring = Ring(S, [AR.alloc([128, T], F32) for _ in range(2)])
            m_ring = Ring(S, [AR.alloc([128, T], F32) for _ in range(4)])
            w_ring = Ring(S, [AR.alloc([128, T], F32) for _ in range(4)])
            xt_ring = Ring(S, [AR.alloc([128, T], F32) for _ in range(4)])
            p_ring = Ring(S, [AR.alloc([128, T], BF16) for _ in range(8)])
            yf_ring = Ring(S, [AR.alloc([128, T], F32) for _ in range(2)])
            sg_ring = Ring(S, [AR.alloc([128, T], F32) for _ in range(2)])
            b_X = Buf(S, "Xt")
            b_IN = Buf(S, "INIT")
            pvr_ring = Ring(S, [])
            pvr_ring.items = [(psum[0], PB[0]), (psum[1], PB[1])]
            pvi_ring = Ring(S, [])
            pvi_ring.items = [(psum[2], PB[2]), (psum[3], PB[3])]
            py_ring = Ring(S, [])
            py_ring.items = [(psum[4], PB[4]), (psum[5], PB[5])]
            for t0 in range(0, SEQ, T):
                S.op("sp", lambda e, t0=t0: e.dma_start(out=xb, in_=xview(src, t0, T)), reads=xs_bufs(t0, T), writes=[bx], dma=True)
                prenorm(i, 1, xb, bx, T, sq, bsq, rs, brs, tmp_ring, [(lambda k: hf[:, k, :], bhf), (lambda k: hb[:, k, :], bhb)])
                for c in range(KC):
                    py, bpy = py_ring.next()
                    for q in range(4):
                        ti = c * 4 + q
                        pvr, bpvr = pvr_ring.next()
                        pvi, bpvi = pvi_ring.next()
                        S.op("pe", lambda e, pvr=pvr, ti=ti, c=c: e.matmul(pvr[:, 0:T], lhsT=BreT[:, ti, :], rhs=hb[:, c, :], start=True, stop=True),
                             reads=[b_W, bhb], writes=[bpvr])
                        S.op("pe", lambda e, pvi=pvi, ti=ti, c=c: e.matmul(pvi[:, 0:T], lhsT=BimT[:, ti, :], rhs=hb[:, c, :], start=True, stop=True),
                             reads=[b_W, bhb], writes=[bpvi])
                        vr, bvr = vr_ring.next()
                        vi, bvi = vi_ring.next()
                        S.op("act", lambda e, vr=vr, pvr=pvr: e.activation(out=vr, in_=pvr[:, 0:T], func=AF.Identity), reads=[bpvr], writes=[bvr])
                        S.op("act", lambda e, vi=vi, pvi=pvi: e.activation(out=vi, in_=pvi[:, 0:T], func=AF.Identity), reads=[bpvi], writes=[bvi])
                        m1, bm1 = m_ring.next()
                        m2, bm2 = m_ring.next()
                        wr, bwr = w_ring.next()
                        S.op("pool", lambda e, m1=m1, vr=vr, ti=ti: e.tensor_tensor(out=m1, in0=vr, in1=Ec[:, ti, :], op=ALU.mult), reads=[bvr, b_E], writes=[bm1])
                        S.op("pool", lambda e, m2=m2, vi=vi, ti=ti: e.tensor_tensor(out=m2, in0=vi, in1=Es[:, ti, :], op=ALU.mult), reads=[bvi, b_E], writes=[bm2])
                        S.op("pool", lambda e, wr=wr, m1=m1, m2=m2: e.tensor_tensor(out=wr, in0=m1, in1=m2, op=ALU.add), reads=[bm1, bm2], writes=[bwr])
                        m3, bm3 = m_ring.next()
                        m4, bm4 = m_ring.next()
                        wi, bwi = w_ring.next()
                        S.op("pool", lambda e, m3=m3, vi=vi, ti=ti: e.tensor_tensor(out=m3, in0=vi, in1=Ec[:, ti, :], op=ALU.mult), reads=[bvi, b_E], writes=[bm3])
                        S.op("pool", lambda e, m4=m4, vr=vr, ti=ti: e.tensor_tensor(out=m4, in0=vr, in1=Es[:, ti, :], op=ALU.mult), reads=[bvr, b_E], writes=[bm4])
                        S.op("pool", lambda e, wi=wi, m3=m3, m4=m4: e.tensor_tensor(out=wi, in0=m3, in1=m4, op=ALU.subtract), reads=[bm3, bm4], writes=[bwi])
                        xtr, bxtr = xt_ring.next()
                        xti, bxti = xt_ring.next()
                        S.op("dve", lambda e, xtr=xtr, wr=wr, ti=ti: e.tensor_tensor_scan(
                            out=xtr, data0=mag[:, ti:ti + 1].to_broadcast([128, T]), data1=wr, initial=INr[:, ti:ti + 1], op0=ALU.mult, op1=ALU.add),
                            reads=[bwr, b_IN, b_p], writes=[bxtr])
                        S.op("dve", lambda e, xti=xti, wi=wi, ti=ti: e.tensor_tensor_scan(
                            out=xti, data0=mag[:, ti:ti + 1].to_broadcast([128, T]), data1=wi, initial=INi[:, ti:ti + 1], op0=ALU.mult, op1=ALU.add),
                            reads=[bwi, b_IN, b_p], writes=[bxti])
                        S.op("act", lambda e, xtr=xtr, ti=ti: e.activation(out=Xtr[:, ti:ti + 1], in_=xtr[:, T - 1:T], func=AF.Identity), reads=[bxtr], writes=[b_X])
                        S.op("act", lambda e, xti=xti, ti=ti: e.activation(out=Xti[:, ti:ti + 1], in_=xti[:, T - 1:T], func=AF.Identity), reads=[bxti], writes=[b_X])
                        p1, bp1 = p_ring.next()
                        p2, bp2 = p_ring.next()
                        p3, bp3 = p_ring.next()
                        p4, bp4 = p_ring.next()
                        S.op("dve", lambda e, p1=p1, xtr=xtr, ti=ti: e.tensor_tensor(out=p1, in0=xtr, in1=Ec[:, ti, :], op=ALU.mult), reads=[bxtr, b_E], writes=[bp1])
                        S.op("dve", lambda e, p2=p2, xti=xti, ti=ti: e.scalar_tensor_tensor(out=p2, in0=xti, scalar=-1.0, in1=Es[:, ti, :], op0=ALU.mult, op1=ALU.mult),
                             reads=[bxti, b_E], writes=[bp2])
                        S.op("dve", lambda e, p3=p3, xtr=xtr, ti=ti: e.tensor_tensor(out=p3, in0=xtr, in1=Es[:, ti, :], op=ALU.mult), reads=[bxtr, b_E], writes=[bp3])
                        S.op("dve", lambda e, p4=p4, xti=xti, ti=ti: e.tensor_tensor(out=p4, in0=xti, in1=Ec[:, ti, :], op=ALU.mult), reads=[bxti, b_E], writes=[bp4])
                        for pi_, (pp, bpp, WT) in enumerate([(p1, bp1, CreT), (p2, bp2, CreT), (p3, bp3, CimTn), (p4, bp4, CimTn)]):
                            S.op("pe", lambda e, py=py, pp=pp, WT=WT, ti=ti, pi_=pi_, q=q: e.matmul(
                                py[:, 0:T], lhsT=WT[:, ti, :], rhs=pp, start=(q == 0 and pi_ == 0), stop=(q == 3 and pi_ == 3)),
                                reads=[b_W, bpp], writes=[bpy])
                    yf, byf = yf_ring.next()
                    S.op("dve", lambda e, yf=yf, py=py, c=c: e.scalar_tensor_tensor(out=yf, in0=hf[:, c, :], scalar=dsk[:, c:c + 1], in1=py[:, 0:T],
                                                                                  op0=ALU.mult, op1=ALU.add), reads=[bhf, bpy, b_p], writes=[byf])
                    S.op("act", lambda e, yf=yf, c=c: e.activation(out=gf[:, c, :], in_=yf, func=AF.Gelu_apprx_tanh), reads=[byf], writes=[bgf])
                    S.op("pool", lambda e, c=c: e.tensor_copy(out=gb[:, c, :], in_=gf[:, c, :]), reads=[bgf], writes=[bgb])
                S.op("dve", lambda e: e.tensor_tensor(out=t32a, in0=cT_, in1=Xtr, op=ALU.mult), reads=[b_X, b_p], writes=[b_p])
                S.op("dve", lambda e: e.tensor_tensor(out=t32b, in0=sT_, in1=Xti, op=ALU.mult), reads=[b_X, b_p], writes=[b_p])
                S.op("dve", lambda e: e.tensor_tensor(out=INr, in0=t32a, in1=t32b, op=ALU.subtract), reads=[b_p], writes=[b_IN])
                S.op("dve", lambda e: e.tensor_tensor(out=t32a, in0=sT_, in1=Xtr, op=ALU.mult), reads=[b_X, b_p], writes=[b_p])
                S.op("dve", lambda e: e.tensor_tensor(out=t32b, in0=cT_, in1=Xti, op=ALU.mult), reads=[b_X, b_p], writes=[b_p])
                S.op("dve", lambda e: e.tensor_tensor(out=INi, in0=t32a, in1=t32b, op=ALU.add), reads=[b_p], writes=[b_IN])
                for n in range(KC):
                    pz, bpz = py_ring.next()
                    for k in range(KC):
                        S.op("pe", lambda e, pz=pz, n=n, k=k: e.matmul(pz[:, 0:T], lhsT=wglu[:, k, n * 128:(n + 1) * 128], rhs=gb[:, k, :],
                                                                       start=(k == 0), stop=(k == KC - 1)), reads=[b_wg, bgb], writes=[bpz])
                    sg, bsg = sg_ring.next()
                    S.op("act", lambda e, sg=sg, pz=pz: e.activation(out=sg, in_=pz[:, 0:T], func=AF.Sigmoid), reads=[bpz], writes=[bsg])
                    S.op("dve", lambda e, sg=sg, n=n: e.tensor_tensor(out=sg, in0=sg, in1=gf[:, n, :], op=ALU.mult), reads=[bsg, bgf], writes=[bsg])
                    S.op("dve", lambda e, sg=sg, n=n: e.scalar_tensor_tensor(out=xb[:, n, :], in0=sg, scalar=gp_col(i, 1, n), in1=xb[:, n, :],
                                                                             op0=ALU.mult, op1=ALU.add), reads=[bsg, b_mod, bx], writes=[bx])
                S.op("sp", lambda e, t0=t0: e.dma_start(out=xview(dst, t0, T), in_=xb), reads=[bx], writes=xs_bufs(t0, T), dma=True)
            cur_src[0] = dst

        def fox_phase(i, dst):
            S.barrier()
            AR.reset()
            W = 512
            src = x_src()
            FT_all = AR.alloc([128, NKB, 16], F32)
            b_FT = Buf(S, "FT")
            CB = AR.alloc([128, 16 * NQB], F32)
            b_CB = Buf(S, "CB")
            sel = AR.alloc([128, 64], F32)
            qgs = AR.alloc([128, 1], F32)
            kgs = AR.alloc([128, 1], F32)
            negb = AR.alloc([16, 1], F32)
            negone = AR.alloc([128, 1], F32)
            Fprev = AR.alloc([16, 1], F32)
            Ccol = AR.alloc([16, NQB], F32)
            ones16 = AR.alloc([16, W], F32)
            ones16b = AR.alloc([16, 2, W], BF16)
            b_k = Buf(S, "foxconst")
            b_Fp = Buf(S, "Fprev")
            b_Cc = Buf(S, "Ccol")
            S.op("sp", lambda e: e.dma_start(out=qgs, in_=fox_qg), writes=[b_k], dma=True)
            S.op("sp", lambda e: e.dma_start(out=kgs, in_=fox_kg), writes=[b_k], dma=True)
            S.op("sp", lambda e: e.dma_start(out=negb, in_=fox_bf), writes=[b_k], dma=True)
            S.op("dve", lambda e: e.tensor_scalar(out=qgs, in0=qgs, scalar1=0.125, scalar2=None, op0=ALU.mult), reads=[b_k], writes=[b_k])
            S.op("dve", lambda e: e.tensor_scalar(out=negb, in0=negb, scalar1=-1.0, scalar2=None, op0=ALU.mult), reads=[b_k], writes=[b_k])
            S.op("dve", lambda e: e.memset(sel[0:64, :], 0.0), writes=[b_k])
            S.op("dve", lambda e: e.memset(sel[64:128, :], 0.0), writes=[b_k])
            S.op("dve", lambda e: e.memset(sel[64:65, :], 1.0), reads=[b_k], writes=[b_k])
            S.op("dve", lambda e: e.memset(Fprev, 0.0), writes=[b_Fp])
            S.op("dve", lambda e: e.memset(negone, -1.0), writes=[b_k])
            S.op("dve", lambda e: e.memset(ones16, 1.0), writes=[b_k])
            S.op("dve", lambda e: e.memset(ones16b, 1.0), writes=[b_k])
            mark = AR.off
            win = AR.alloc([128, KC, 3088], BF16)
            b_win_sb = Buf(S)
            for k in range(KC):
                S.op("sp", lambda e, k=k: e.dma_start(out=win[:, k, :], in_=foxin_bf[k * 128:(k + 1) * 128, :]),
                     reads=[b_mixw["foxin"]], writes=[b_win_sb], dma=True)
            xb = AR.alloc([128, KC, W], F32)
            bx = Buf(S)
            h = AR.alloc([128, KC, W], BF16)
            bh = Buf(S)
            sq = AR.alloc([128, KC, W], BF16)
            bsq = Buf(S)
            rs = AR.alloc([128, W], F32)
            brs = Buf(S)
            tmp_ring = Ring(S, [AR.alloc([128, W], F32) for _ in range(2)])
            qs_ring = Ring(S, [AR.alloc([128, W], BF16) for _ in range(2)])
            rq_ring = Ring(S, [AR.alloc([128, W], F32) for _ in range(2)])
            qn_ring = Ring(S, [AR.alloc([128, W], BF16) for _ in range(3)])
            vsb_ring = Ring(S, [AR.alloc([128, 1024], BF16) for _ in range(2)])
            lf = AR.alloc([16, W], F32)
            Fblk = AR.alloc([16, W], F32)
            Frel = AR.alloc([16, W], F32)
            hif = AR.alloc([16, W], F32)
            hib = AR.alloc([16, W], BF16)
            lob = AR.alloc([16, W], BF16)
            b_f = Buf(S, "fwork")
            pq_ring = Ring(S, [])
            pq_ring.items = [(psum[0], PB[0]), (psum[1], PB[1])]
            pss_ring = Ring(S, [])
            pss_ring.items = [(psum[2], PB[2]), (psum[3], PB[3])]
            pv_ring = Ring(S, [])
            pv_ring.items = [(psum[4], PB[4]), (psum[5], PB[5])]
            pf, bpf = psum[7], PB[7]
            b_qT = Buf(S, "qTa")
            b_kT = Buf(S, "kTa")
            b_vS = Buf(S, "vS")
            for bi, t0 in enumerate(range(0, SEQ, W)):
                S.op("sp", lambda e, t0=t0: e.dma_start(out=xb, in_=xview(src, t0, W)), reads=xs_bufs(t0, W), writes=[bx], dma=True)
                prenorm(i, 1, xb, bx, W, sq, bsq, rs, brs, tmp_ring, [(lambda k: h[:, k, :], bh)])
                for (coff, gsc, dT, bdT) in ([] if 'a' in FOX_DBG else [(0, qgs, qTa, b_qT), (D, kgs, kTa, b_kT)]):
                    for m in range(KC):
                        pq, bpq = pq_ring.next()
                        for k in range(KC):
                            S.op("pe", lambda e, pq=pq, k=k, m=m, coff=coff: e.matmul(
                                pq[:, :], lhsT=win[:, k, coff + m * 128:coff + (m + 1) * 128], rhs=h[:, k, :], start=(k == 0), stop=(k == KC - 1)),
                                reads=[b_win_sb, bh], writes=[bpq])
                        qs, bqs = qs_ring.next()
                        S.op("act", lambda e, qs=qs, pq=pq: e.activation(out=qs, in_=pq[:, :], func=AF.Square), reads=[bpq], writes=[bqs])
                        pss, bpss = pss_ring.next()
                        S.op("pe", lambda e, pss=pss, qs=qs: e.matmul(pss[:, :], lhsT=bones_bf, rhs=qs, start=True, stop=True), reads=[bqs, b_c], writes=[bpss])
                        rq, brq = rq_ring.next()
                        S.op("act", lambda e, rq=rq, pss=pss: e.activation(out=rq, in_=pss[:, :], func=AF.Sqrt, bias=EPS, scale=1.0 / 64), reads=[bpss], writes=[brq])
                        S.op("dve", lambda e, rq=rq: e.reciprocal(out=rq, in_=rq), reads=[brq], writes=[brq])
                        qn, bqn = qn_ring.next()
                        S.op("dve", lambda e, qn=qn, pq=pq, rq=rq, gsc=gsc: e.scalar_tensor_tensor(out=qn, in0=pq[:, :], scalar=gsc[:, 0:1], in1=rq,
                                                                                               op0=ALU.mult, op1=ALU.mult), reads=[bpq, brq, b_k], writes=[bqn])
                        for hh in range(2):
                            S.op("sp", lambda e, qn=qn, hh=hh, m=m, t0=t0, dT=dT: e.dma_start(out=dT[2 * m + hh, 0:64, t0:t0 + W], in_=qn[hh * 64:(hh + 1) * 64, :]),
                                 reads=[bqn], writes=[bdT], dma=True)
                for tt in range(0 if 'b' in FOX_DBG else 4):
                    vsb, bvsb = vsb_ring.next()
                    for hf_ in range(2):
                        pv, bpv = pv_ring.next()
                        for k in range(KC):
                            S.op("pe", lambda e, pv=pv, k=k, tt=tt, hf_=hf_: e.matmul(
                                pv[:, :], lhsT=h[:, k, tt * 128:(tt + 1) * 128], rhs=win[:, k, 2 * D + hf_ * 512:2 * D + (hf_ + 1) * 512],
                                start=(k == 0), stop=(k == KC - 1)), reads=[b_win_sb, bh], writes=[bpv])
                        if hf_ == 0:
                            S.op("act", lambda e, pv=pv, vsb=vsb: e.activation(out=vsb[:, 0:512], in_=pv[:, :], func=AF.Identity), reads=[bpv], writes=[bvsb])
                        else:
                            S.op("dve", lambda e, pv=pv, vsb=vsb: e.tensor_copy(out=vsb[:, 512:1024], in_=pv[:, :]), reads=[bpv], writes=[bvsb])
                    kb = t0 // 128 + tt
                    S.op("sp", lambda e, vsb=vsb, kb=kb: e.dma_start(out=vS[:, kb].rearrange("h p d -> p h d"), in_=vsb.rearrange("p (h d) -> p h d", h=16)),
                         reads=[bvsb], writes=[b_vS], dma=True)
                if 'c' in FOX_DBG:
                    continue
                for k in range(KC):
                    S.op("pe", lambda e, k=k: e.matmul(pf[0:16, :], lhsT=win[:, k, 3 * D:3 * D + 16], rhs=h[:, k, :], start=(k == 0), stop=(k == KC - 1)),
                         reads=[b_win_sb, bh], writes=[bpf])
                S.op("act", lambda e: e.activation(out=lf, in_=pf[0:16, :], func=AF.Exp, bias=negb[:, 0:1], scale=-1.0), reads=[bpf, b_k], writes=[b_f])
                S.op("act", lambda e: e.activation(out=lf, in_=lf, func=AF.Ln, bias=1.0, scale=1.0), reads=[b_f], writes=[b_f])
                S.op("dve", lambda e: e.tensor_tensor_scan(out=Fblk, data0=ones16, data1=lf, initial=Fprev[:, 0:1], op0=ALU.mult, op1=ALU.subtract),
                     reads=[b_f, b_Fp, b_k], writes=[b_f])
                S.op("dve", lambda e: e.tensor_copy(out=Fprev, in_=Fblk[:, W - 1:W]), reads=[b_f], writes=[b_Fp])
                S.op("dve", lambda e, bi=bi: e.tensor_copy(out=Ccol[:, bi:bi + 1], in_=Fblk[:, 0:1]), reads=[b_f], writes=[b_Cc])
                S.op("dve", lambda e: e.tensor_scalar(out=Frel, in0=Fblk, scalar1=Fblk[:, 0:1], scalar2=None, op0=ALU.subtract), reads=[b_f], writes=[b_f])
                S.op("dve", lambda e: e.tensor_copy(out=hib, in_=Frel), reads=[b_f], writes=[b_f])
                S.op("dve", lambda e: e.tensor_copy(out=hif, in_=hib), reads=[b_f], writes=[b_f])
                S.op("dve", lambda e: e.tensor_tensor(out=lob, in0=Frel, in1=hif, op=ALU.subtract), reads=[b_f], writes=[b_f])
                S.op("sp", lambda e, t0=t0: e.dma_start(out=qTa[:, 64, t0:t0 + W], in_=hib), reads=[b_f], writes=[b_qT], dma=True)
                S.op("sp", lambda e, t0=t0: e.dma_start(out=qTa[:, 65, t0:t0 + W], in_=lob), reads=[b_f], writes=[b_qT], dma=True)
                S.op("sp", lambda e, t0=t0: e.dma_start(out=kTa[:, 64:66, t0:t0 + W], in_=ones16b), reads=[b_k], writes=[b_kT], dma=True)
                for tt in range(4):
                    pt_, bpt_ = pv_ring.next()
                    kb = t0 // 128 + tt
                    S.op("pe", lambda e, pt_=pt_, tt=tt: e.transpose(out=pt_[:, 0:16], in_=Fblk[0:16, tt * 128:(tt + 1) * 128], identity=ident[0:16, 0:16]),
                         reads=[b_f, b_c], writes=[bpt_])
                    S.op("act", lambda e, pt_=pt_, kb=kb: e.activation(out=FT_all[:, kb, :], in_=pt_[:, 0:16], func=AF.Identity), reads=[bpt_], writes=[b_FT])
            if DEBUG == 3:
                b_dbg = Buf(S)
                cvt = AR.alloc([128, 512], F32)
                for n_, (srcap, bsrc) in enumerate([(h[:, 0, :], bh), (win[:, 0, 0:512], b_win_sb), (win[:, 3, 2048:2560], b_win_sb), (sq[:, 0, :], bsq)]):
                    S.op("dve", lambda e, srcap=srcap: e.tensor_copy(out=cvt, in_=srcap), reads=[bsrc, b_dbg], writes=[b_dbg])
                    o_ = n_ * 512
                    S.op("sp", lambda e, o_=o_: e.dma_start(out=dbg[:, o_:o_ + 512], in_=cvt), reads=[b_dbg], writes=[b_dbg], dma=True)
                S.op("sp", lambda e: e.dma_start(out=dbg[:, 2048:2560], in_=xb[:, 0, :]), reads=[bx], writes=[b_dbg], dma=True)
                S.op("sp", lambda e: e.dma_start(out=dbg[:, 2560:3072], in_=rs), reads=[brs], writes=[b_dbg], dma=True)
            b_cD = Buf(S, "cD")
            S.op("sp", lambda e: e.dma_start(out=cD, in_=Ccol), reads=[b_Cc], writes=[b_cD], dma=True)
            S.op("sp", lambda e: e.dma_start(out=CB, in_=cD.rearrange("h i -> (h i)").unsqueeze(0).to_broadcast([128, 16 * NQB])),
                 reads=[b_cD], writes=[b_CB], dma=True)
            S.barrier()
            AR.off = mark
            QT_ring = Ring(S, [AR.alloc([128, SEQ], BF16) for _ in range(2)])
            KT_ring = Ring(S, [AR.alloc([128, SEQ], BF16) for _ in range(2)])
            VA_ring = Ring(S, [AR.alloc([128, NKB, 128], BF16) for _ in range(2)])
            Bias_ring = Ring(S, [AR.alloc([128, NKB, NQB], F32) for _ in range(2)])
            pt_ring = Ring(S, [AR.alloc([128, W], BF16) for _ in range(4)])
            osb_ring = Ring(S, [AR.alloc([128, W], F32) for _ in range(2)])
            rd_ring = Ring(S, [AR.alloc([64, W], F32) for _ in range(2)])
            on_ring = Ring(S, [AR.alloc([64, W], BF16) for _ in range(2)])
            for (va, bva) in VA_ring.items:
                S.op("pool", lambda e, va=va: e.memset(va, 1.0), writes=[bva])
            for (qt, bqt) in QT_ring.items + KT_ring.items:
                S.op("pool", lambda e, qt=qt: e.memset(qt[64:128, :], 0.0), writes=[bqt])
            ps_ring = Ring(S, [])
            ps_ring.items = [(psum[q], PB[q]) for q in range(4)]
            po_ring = Ring(S, [])
            po_ring.items = [(psum[4], PB[4]), (psum[5], PB[5])]
            b_oT = Buf(S, "oTs")
            for hd in range({1: 0, 2: 1, 3: 16, 5: 16}[FOX_STAGE]):
                QT, bQT = QT_ring.next()
                KT, bKT = KT_ring.next()
                VA, bVA = VA_ring.next()
                Bs, bBs = Bias_ring.next()
                S.op("sp", lambda e, QT=QT, hd=hd: e.dma_start(out=QT[0:66, :], in_=qTa[hd]), reads=[b_qT], writes=[bQT], dma=True)
                S.op("sp", lambda e, KT=KT, hd=hd: e.dma_start(out=KT[0:66, :], in_=kTa[hd]), reads=[b_kT], writes=[bKT], dma=True)
                for k0 in range(0, NKB, 8):
                    S.op("sp", lambda e, VA=VA, hd=hd, k0=k0: e.dma_start(out=VA[:, k0:k0 + 8, 0:64], in_=vS[hd, k0:k0 + 8].rearrange("kb p d -> p kb d")),
                         reads=[b_vS], writes=[bVA], dma=True)
                for ib in range(NQB):
                    S.op("dve", lambda e, Bs=Bs, ib=ib, hd=hd: e.tensor_scalar(out=Bs[:, :, ib], in0=FT_all[:, :, hd], scalar1=negone[:, 0:1],
                                                                               scalar2=CB[:, hd * NQB + ib:hd * NQB + ib + 1], op0=ALU.mult, op1=ALU.add),
                         reads=[b_FT, b_CB, b_k], writes=[bBs])
                for ib in range(NQB if FOX_STAGE != 5 else 0):
                    nkb = 4 * (ib + 1)
                    po, bpo = po_ring.next()
                    for j in range(nkb):
                        jj = j - 4 * ib
                        qlo = 128 * jj if jj >= 0 else 0
                        N = W - qlo
                        ps, bps = ps_ring.next()
                        S.op("pe", lambda e, ps=ps, KT=KT, QT=QT, j=j, ib=ib, qlo=qlo, N=N: e.matmul(
                            ps[:, 0:N], lhsT=KT[:, j * 128:(j + 1) * 128], rhs=QT[:, ib * W + qlo:(ib + 1) * W], start=True, stop=True),
                            reads=[bKT, bQT], writes=[bps])
                        pt, bpt = pt_ring.next()
                        S.op("act", lambda e, pt=pt, ps=ps, Bs=Bs, j=j, ib=ib, N=N: e.activation(
                            out=pt[:, 0:N], in_=ps[:, 0:N], func=AF.Exp, bias=Bs[:, j, ib:ib + 1], scale=1.0), reads=[bps, bBs], writes=[bpt])
                        if jj >= 0:
                            S.op("pool", lambda e, pt=pt: e.tensor_tensor(out=pt[:, 0:128], in0=pt[:, 0:128], in1=tri_bf, op=ALU.mult),
                                 reads=[bpt, b_c], writes=[bpt])
                        S.op("pe", lambda e, po=po, VA=VA, pt=pt, j=j, qlo=qlo, N=N, nkb=nkb: e.matmul(
                            po[:, qlo:W], lhsT=VA[:, j, :], rhs=pt[:, 0:N], start=(j == 0), stop=(j == nkb - 1)),
                            reads=[bVA, bpt], writes=[bpo])
                    osb, bosb = osb_ring.next()
                    S.op("act", lambda e, osb=osb, po=po: e.activation(out=osb, in_=po[:, :], func=AF.Identity), reads=[bpo], writes=[bosb])
                    rd, brd = rd_ring.next()
                    S.op("dve", lambda e, rd=rd, osb=osb: e.tensor_copy(out=rd, in_=osb[64:128, :]), reads=[bosb], writes=[brd])
                    S.op("dve", lambda e, rd=rd: e.reciprocal(out=rd, in_=rd), reads=[brd], writes=[brd])
                    on, bon = on_ring.next()
                    S.op("dve", lambda e, on=on, osb=osb, rd=rd: e.tensor_tensor(out=on, in0=osb[0:64, :], in1=rd, op=ALU.mult), reads=[bosb, brd], writes=[bon])
                    S.op("sp", lambda e, on=on, hd=hd, ib=ib: e.dma_start(out=oTs[hd, :, ib * W:(ib + 1) * W], in_=on), reads=[bon], writes=[b_oT], dma=True)
            if DEBUG == 2:
                b_dbg = Buf(S)
                cvt = AR.alloc([128, 512], F32)
                S.op("sp", lambda e: e.dma_start(out=dbg[:, 0:32], in_=CB[:, 0:32]), reads=[b_CB], writes=[b_dbg], dma=True)
                S.op("sp", lambda e: e.dma_start(out=dbg[:, 32:160], in_=FT_all.rearrange("p a b -> p (a b)")[:, 0:128]), reads=[b_FT], writes=[b_dbg], dma=True)
                for n_, (srcap, bsrc) in enumerate([(QT_ring.items[1][0][:, 0:512], QT_ring.items[1][1]), (KT_ring.items[1][0][:, 0:512], KT_ring.items[1][1]),
                                                   (osb_ring.items[1][0], osb_ring.items[1][1]), (VA_ring.items[1][0].rearrange("p a b -> p (a b)")[:, 0:512], VA_ring.items[1][1])]):
                    S.op("dve", lambda e, srcap=srcap: e.tensor_copy(out=cvt, in_=srcap), reads=[bsrc, b_dbg], writes=[b_dbg])
                    o_ = 160 + n_ * 512
                    S.op("sp", lambda e, o_=o_: e.dma_start(out=dbg[:, o_:o_ + 512], in_=cvt), reads=[b_dbg], writes=[b_dbg], dma=True)
                S.op("sp", lambda e: e.dma_start(out=dbg[:, 2208:2272], in_=Bias_ring.items[1][0].rearrange("p a b -> p (a b)")[:, 0:16].to_broadcast([128, 16]) if False else Bias_ring.items[1][0].rearrange("p a b -> p (a b)")[:, 0:16]), reads=[Bias_ring.items[1][1]], writes=[b_dbg], dma=True) if False else None
            S.barrier()
            AR.off = mark
            xc = AR.alloc([128, KC, W], F32)
            bxc = Buf(S)
            oc = AR.alloc([128, KC, W], BF16)
            boc = Buf(S)
            wo = AR.alloc([128, KC, D], BF16)
            bwo_sb = Buf(S)
            S.op("sp", lambda e: e.dma_start(out=wo, in_=foxo_bf.rearrange("(k p) n -> p k n", p=128)), reads=[b_mixw["foxo"]], writes=[bwo_sb], dma=True)
            py_ring = Ring(S, [])
            py_ring.items = [(psum[4], PB[4]), (psum[5], PB[5])]
            oT2 = oTs.rearrange("h d t -> (h d) t")
            for t0 in range(0, SEQ, W):
                S.op("sp", lambda e, t0=t0: e.dma_start(out=xc, in_=xview(src, t0, W)), reads=xs_bufs(t0, W), writes=[bxc], dma=True)
                S.op("sp", lambda e, t0=t0: e.dma_start(out=oc, in_=oT2[:, t0:t0 + W].rearrange("(c p) t -> p c t", p=128)), reads=[b_oT], writes=[boc], dma=True)
                out_proj_residual(i, wo, bwo_sb, lambda k: oc[:, k, :], boc, xc, bxc, W, py_ring)
                S.op("sp", lambda e, t0=t0: e.dma_start(out=xview(dst, t0, W), in_=xc), reads=[bxc], writes=xs_bufs(t0, W), dma=True)
            cur_src[0] = dst

        def copy_phase(dst):
            S.barrier()
            AR.reset()
            src = x_src()
            xb = AR.alloc([128, KC, 512], F32)
            bx = Buf(S)
            for t0 in range(0, SEQ, 512):
                S.op("sp", lambda e, t0=t0: e.dma_start(out=xb, in_=xview(src, t0, 512)),
                     reads=xs_bufs(t0, 512), writes=[bx], dma=True)
                S.op("sp", lambda e, t0=t0: e.dma_start(out=xview(dst, t0, 512), in_=xb),
                     reads=[bx], writes=xs_bufs(t0, 512), dma=True)
            cur_src[0] = dst

        nphase = []
        for i in range(nlayers):
            if do_ffn:
                nphase.append(("ffn", i, 0, 0))
            if (i % 4) in mixers:
                nphase.append(("mix", i, i % 4, 1))
            if do_ffn:
                nphase.append(("ffn", i, 1, 2))
        if not nphase:
            nphase.append(("copy", 0, 0, 0))
        for pi, (kind, i, a, s) in enumerate(nphase):
            dst = outT if pi == len(nphase) - 1 else xs
            if kind == "ffn":
                ffn_phase(i, a, s, dst)
            elif kind == "copy":
                copy_phase(dst)
            else:
                if a == 0:
                    pool_phase(i, dst)
                elif a == 3:
                    conv_phase(i, dst)
                elif a == 2:
                    s5_phase(i, dst)
                else:
                    fox_phase(i, dst)
        S.barrier()
        S.emit(st)
    return nc


def _fm(v):
    return np.ascontiguousarray(np.asarray(v, np.float32).reshape(KC, 128).T)


def prep_inputs(inp, SEQ):
    f = lambda a: np.ascontiguousarray(np.asarray(a, dtype=np.float32))
    HALF = SEQ // 2
    x = f(inp["x"])
    c_all = f(inp["c"])
    shared = {}
    shared["ada_bT"] = np.ascontiguousarray(np.concatenate([f(inp["ada_b"])[i].reshape(72, 128).T for i in range(4)], axis=1))
    ng = f(inp["norm_g"])
    shared["norm_gT"] = np.ascontiguousarray(np.concatenate([_fm(ng[i, s]) for i in range(4) for s in range(3)], axis=1))
    shared["cTall"] = np.ascontiguousarray(c_all.reshape(4, KC, 128).transpose(2, 1, 0))
    shared["pool_sT"] = _fm(inp["pool_scale"][0])
    icnt = np.zeros((128, KC, 16), np.float32)
    for k in range(KC):
        w = 2 << (k // 2)
        icnt[:, k, :] = 1.0 / np.minimum(np.arange(16) + 1, w)
    shared["pool_icnt"] = icnt
    shared["fox_bf"] = f(inp["fox_b_f"])[0].reshape(16, 1)
    shared["fox_qg"] = np.ascontiguousarray(np.tile(f(inp["fox_q_gain"])[0], 2).reshape(128, 1))
    shared["fox_kg"] = np.ascontiguousarray(np.tile(f(inp["fox_k_gain"])[0], 2).reshape(128, 1))
    shared["s5_lre"] = np.ascontiguousarray(f(inp["s5_lam_re"])[0].reshape(32, 128).T)
    shared["s5_lim"] = np.ascontiguousarray(f(inp["s5_lam_im"])[0].reshape(32, 128).T)
    shared["s5_ldt"] = np.ascontiguousarray(np.repeat(f(inp["s5_log_dt"])[0], 64).reshape(32, 128).T)
    t16 = lambda a: np.ascontiguousarray(a.reshape(32, 128, 16).transpose(1, 0, 2))
    shared["s5_bre"] = t16(f(inp["s5_b_re"])[0].reshape(4096, 16))
    shared["s5_bim"] = t16(f(inp["s5_b_im"])[0].reshape(4096, 16))
    shared["s5_cre"] = t16(f(inp["s5_c_re"])[0].transpose(0, 2, 1).reshape(4096, 16))
    shared["s5_cim"] = t16(f(inp["s5_c_im"])[0].transpose(0, 2, 1).reshape(4096, 16))
    shared["s5_dT"] = _fm(inp["s5_d"][0])
    cw = f(inp["conv_w"])[0]
    shared["conv_wT"] = np.ascontiguousarray(np.concatenate([_fm(cw[j, 0]) for j in range(3)], axis=1))
    shared["ident"] = np.eye(128, dtype=np.float32)
    shared["tri"] = np.triu(np.ones((128, 128), np.float32))
    bo = np.zeros((128, 128), np.float32)
    bo[:64, :64] = 1
    bo[64:, 64:] = 1
    shared["bones"] = bo
    shared["iota"] = np.ascontiguousarray(np.tile(np.arange(256, dtype=np.float32), (128, 1)))
    mixw = np.concatenate([f(inp["fox_w_in"])[0], f(inp["fox_w_o"])[0], f(inp["s5_w_glu"])[0], f(inp["conv_w_in"])[0],
                           f(inp["conv_w_out"])[0], f(inp["pool_w"])[0].reshape(1024, 256)], axis=1)
    ada_w = inp["ada_w"]
    w_in = inp["ffn_w_in"]
    w_out = inp["ffn_w_out"]
    maps = []
    for c in range(NCORES):
        b, hf = c // 2, c % 2
        m = dict(shared)
        m["xTh"] = np.ascontiguousarray(x[b, hf * HALF:(hf + 1) * HALF].T)
        bs = np.zeros((128, 4), np.float32)
        bs[:, b] = 1.0
        m["bsel"] = bs
        m["ada_w_s"] = f(ada_w[:, :, c * 1152:(c + 1) * 1152])
        m["win_s"] = f(w_in[c // 2, c % 2])
        m["wout_s"] = f(w_out[c // 2, c % 2])
        m["mixw_s"] = np.ascontiguousarray(mixw[c * 128:(c + 1) * 128])
        maps.append(m)
    return maps


_NC_CACHE = {}


def assemble(res, B, SEQ):
    HALF = SEQ // 2
    out = np.empty((B, SEQ, D), np.float32)
    for b in range(B):
        o = res.results[2 * b]["outT"]
        for r in range(2):
            out[b, r * HALF:(r + 1) * HALF] = o[:, r].reshape(D, HALF).T
    return out


def kernel(**inputs):
    x = np.asarray(inputs["x"])
    B, SEQ, _ = x.shape
    key = ("full", SEQ)
    if key not in _NC_CACHE:
        _NC_CACHE[key] = build(SEQ)
    nc = _NC_CACHE[key]
    maps = prep_inputs(inputs, SEQ)
    res = run_bass_kernel_spmd(nc, maps, core_ids=list(range(NCORES)))
    return assemble(res, B, SEQ)
```
